# Optimizing a Trainium2 kernel written in Bass

```python
import math
import jax, jax.numpy as jnp
from jax import lax
import numpy as np

D_MODEL = 1024
BATCH = 32
SEQ = 2048
DEPTH = 2

N_MIXERS = 2
N_META = 16
N_HEADS = 8
HEAD_DIM = 64
V_DIM = 2 * HEAD_DIM
QKV_DIM = 3 * N_HEADS * 2 * HEAD_DIM
ROPE_THETA = 10000.0
CONV_WIDTH = 3
D_FF = -(-8 * D_MODEL // (3 * 256)) * 256
Q_BLOCK = 128
EPS = 1e-6
N_ATTN_LAYERS = (DEPTH + 1) // 2
N_CONV_LAYERS = DEPTH // 2

kernel_name = "hybrid_diffattn_shortconv_meta"


def rms_norm(x, g):
    xf = x.astype(jnp.float32)
    y = xf * lax.rsqrt(jnp.mean(xf * xf, axis=-1, keepdims=True) + EPS)
    return (y * g.astype(jnp.float32)).astype(x.dtype)


def rope_tables(length):
    inv = 1.0 / (ROPE_THETA ** (jnp.arange(0, HEAD_DIM, 2, dtype=jnp.float32) / HEAD_DIM))
    pos = jnp.arange(length, dtype=jnp.float32)
    ang = pos[:, None] * inv[None, :]
    return jnp.cos(ang), jnp.sin(ang)


def apply_rope(x, cos, sin):
    xf = x.astype(jnp.float32)
    x1, x2 = jnp.split(xf, 2, axis=-1)
    out = jnp.concatenate([x1 * cos - x2 * sin, x2 * cos + x1 * sin], axis=-1)
    return out.astype(x.dtype)


def diff_attention(h, w_qkv, q_gain, k_gain, lam_q1, lam_k1, lam_q2, lam_k2,
                   sub_gain, w_o, lambda_init):
    b, length, _ = h.shape
    qkv = h @ w_qkv
    q, k, v = jnp.split(qkv, 3, axis=-1)
    q = q.reshape(b, length, N_HEADS, 2, HEAD_DIM).transpose(0, 2, 3, 1, 4)
    k = k.reshape(b, length, N_HEADS, 2, HEAD_DIM).transpose(0, 2, 3, 1, 4)
    v = v.reshape(b, length, N_HEADS, V_DIM).transpose(0, 2, 1, 3)
    cos, sin = rope_tables(length)
    q = apply_rope(rms_norm(q, q_gain), cos, sin) * (HEAD_DIM ** -0.5)
    k = apply_rope(rms_norm(k, k_gain), cos, sin)
    lam = (jnp.exp(jnp.sum(lam_q1.astype(jnp.float32) * lam_k1.astype(jnp.float32)))
           - jnp.exp(jnp.sum(lam_q2.astype(jnp.float32) * lam_k2.astype(jnp.float32)))
           + lambda_init)
    bounds = [(0, N_META)] + [(s, s + Q_BLOCK) for s in range(N_META, length, Q_BLOCK)]
    outs = []
    for qs, qe in bounds:
        qb = q[:, :, :, qs:qe]
        kb = k[:, :, :, :qe]
        vb = v[:, :, :qe]
        s = jnp.einsum('bhcqd,bhckd->bhcqk', qb, kb).astype(jnp.float32)
        mask = jnp.arange(qs, qe)[:, None] >= jnp.arange(qe)[None, :]
        s = jnp.where(mask, s, -jnp.inf)
        p = jax.nn.softmax(s, axis=-1)
        a = p[:, :, 0] - lam * p[:, :, 1]
        outs.append(jnp.einsum('bhqk,bhkd->bhqd', a.astype(vb.dtype), vb))
    o = jnp.concatenate(outs, axis=2)
    o = rms_norm(o, sub_gain) * (1.0 - lambda_init)
    o = o.transpose(0, 2, 1, 3).reshape(b, length, N_HEADS * V_DIM)
    return o @ w_o


def short_conv(h, w_in, conv_w, w_out):
    bcu = h @ w_in
    gate_b, gate_c, u = jnp.split(bcu, 3, axis=-1)
    z = gate_c * u
    zc = lax.conv_general_dilated(
        z, conv_w.astype(z.dtype), window_strides=(1,),
        padding=[(CONV_WIDTH - 1, 0)],
        dimension_numbers=('NWC', 'WIO', 'NWC'),
        feature_group_count=D_MODEL)
    return (gate_b * zc) @ w_out


def swiglu(h, w_gate_up, w_down):
    g, u = jnp.split(h @ w_gate_up, 2, axis=-1)
    return (jax.nn.silu(g) * u) @ w_down


def setup_inputs(seed: int = 0) -> dict:
    key = jax.random.key(seed)
    ks = jax.random.split(key, 20)
    f32 = jnp.float32
    nrm = lambda k, shape, scale: jax.random.normal(k, shape, f32) * scale
    gain = lambda k, shape: 1.0 + 0.02 * jax.random.normal(k, shape, f32)
    return {
        "x": nrm(ks[0], (BATCH, SEQ, D_MODEL), 1.0),
        "meta_tokens": nrm(ks[1], (N_META, D_MODEL), 1.0),
        "mixer_norm_g": gain(ks[2], (DEPTH, D_MODEL)),
        "ffn_norm_g": gain(ks[3], (DEPTH, D_MODEL)),
        "attn_w_qkv": nrm(ks[4], (N_ATTN_LAYERS, D_MODEL, QKV_DIM), D_MODEL ** -0.5),
        "attn_q_gain": gain(ks[5], (N_ATTN_LAYERS, HEAD_DIM)),
        "attn_k_gain": gain(ks[6], (N_ATTN_LAYERS, HEAD_DIM)),
        "attn_lambda_q1": nrm(ks[7], (N_ATTN_LAYERS, HEAD_DIM), 0.1),
        "attn_lambda_k1": nrm(ks[8], (N_ATTN_LAYERS, HEAD_DIM), 0.1),
        "attn_lambda_q2": nrm(ks[9], (N_ATTN_LAYERS, HEAD_DIM), 0.1),
        "attn_lambda_k2": nrm(ks[10], (N_ATTN_LAYERS, HEAD_DIM), 0.1),
        "attn_sub_gain": gain(ks[11], (N_ATTN_LAYERS, V_DIM)),
        "attn_w_o": nrm(ks[12], (N_ATTN_LAYERS, N_HEADS * V_DIM, D_MODEL), (N_HEADS * V_DIM) ** -0.5),
        "conv_w_in": nrm(ks[13], (N_CONV_LAYERS, D_MODEL, 3 * D_MODEL), D_MODEL ** -0.5),
        "conv_w": nrm(ks[14], (N_CONV_LAYERS, CONV_WIDTH, 1, D_MODEL), CONV_WIDTH ** -0.5),
        "conv_w_out": nrm(ks[15], (N_CONV_LAYERS, D_MODEL, D_MODEL), D_MODEL ** -0.5),
        "ffn_w_gate_up": nrm(ks[16], (DEPTH, D_MODEL, 2 * D_FF), D_MODEL ** -0.5),
        "ffn_w_down": nrm(ks[17], (DEPTH, D_FF, D_MODEL), D_FF ** -0.5),
    }


def reference(x, meta_tokens, mixer_norm_g, ffn_norm_g, attn_w_qkv, attn_q_gain,
              attn_k_gain, attn_lambda_q1, attn_lambda_k1, attn_lambda_q2,
              attn_lambda_k2, attn_sub_gain, attn_w_o, conv_w_in, conv_w,
              conv_w_out, ffn_w_gate_up, ffn_w_down):
    b = x.shape[0]
    meta = jnp.broadcast_to(meta_tokens.astype(x.dtype)[None], (b, N_META, D_MODEL))
    h = jnp.concatenate([meta, x], axis=1)
    for i in range(DEPTH):
        hn = rms_norm(h, mixer_norm_g[i])
        j = i // N_MIXERS
        if i % N_MIXERS == 0:
            lambda_init = 0.8 - 0.6 * math.exp(-0.3 * i)
            h = h + diff_attention(hn, attn_w_qkv[j], attn_q_gain[j], attn_k_gain[j],
                                   attn_lambda_q1[j], attn_lambda_k1[j],
                                   attn_lambda_q2[j], attn_lambda_k2[j],
                                   attn_sub_gain[j], attn_w_o[j], lambda_init)
        else:
            h = h + short_conv(hn, conv_w_in[j], conv_w[j], conv_w_out[j])
        h = h + swiglu(rms_norm(h, ffn_norm_g[i]), ffn_w_gate_up[i], ffn_w_down[i])
    return h[:, N_META:]
```

```python
import math
import numpy as np
import concourse.bass as bass
import concourse.mybir as mybir
from concourse.bass_utils import run_bass_kernel_spmd

F32 = mybir.dt.float32
BF16 = mybir.dt.bfloat16
AF = mybir.ActivationFunctionType
ALU = mybir.AluOpType
AX = mybir.AxisListType

D = 1024
KC = 8
H = 8
DFF = 2816
FC = 22
NMETA = 16
EPS = 1e-6
LAMBDA_INIT0 = 0.8 - 0.6 * math.exp(-0.3 * 0)
N_CORES = 8
DEBUG_ATTN_STAGE = 9
DEBUG_ROPE = 9


class Op:
    __slots__ = ("eng", "fn", "deps", "dma", "signal", "sem", "val", "idx")


class Prog:
    SEM_ROT = 20000
    NDMA_SEM = 24

    def __init__(self, nc):
        self.nc = nc
        self.ops = []
        self.spaces = {}
        self.eng = {"pe": nc.tensor, "act": nc.scalar, "dve": nc.vector, "pool": nc.gpsimd,
                    "sp": nc.sync}
        self.dma_hist = {"sp": [], "pool": [], "act": []}

    def add(self, eng, fn, reads=(), writes=(), dma=False):
        op = Op()
        op.eng = eng
        op.fn = fn
        op.dma = dma
        op.signal = False
        op.sem = None
        op.val = 0
        op.idx = len(self.ops)
        op.deps = set()
        for (sp, lo, hi) in reads:
            self._access(op, sp, lo, hi, False)
        for (sp, lo, hi) in writes:
            self._access(op, sp, lo, hi, True)
        op.deps.discard(op)
        if dma:
            hist = self.dma_hist[eng]
            if len(hist) >= self.NDMA_SEM:
                op.deps.add(hist[len(hist) - self.NDMA_SEM])
            hist.append(op)
        self.ops.append(op)
        return op

    def _access(self, op, space, lo, hi, is_write):
        recs = self.spaces.get(space)
        if recs is None:
            recs = []
            self.spaces[space] = recs
        deps = op.deps
        if is_write:
            new = []
            for r in recs:
                if r[1] <= lo or r[0] >= hi:
                    new.append(r)
                    continue
                if r[2] is not None:
                    deps.add(r[2])
                deps.update(r[3])
                if r[0] < lo:
                    new.append([r[0], lo, r[2], list(r[3])])
                if r[1] > hi:
                    new.append([hi, r[1], r[2], list(r[3])])
            new.append([lo, hi, op, []])
            self.spaces[space] = new
        else:
            covered = lo
            for r in recs:
                if r[1] <= lo or r[0] >= hi:
                    continue
                if r[2] is not None:
                    deps.add(r[2])
                if op.dma:
                    r[3].append(op)
                else:
                    r[3] = [x for x in r[3] if x.dma or x.eng != op.eng] + [op]
            self._fill_read_gaps(space, lo, hi, op)

    def _fill_read_gaps(self, space, lo, hi, op):
        recs = self.spaces[space]
        ivs = sorted((r[0], r[1]) for r in recs if not (r[1] <= lo or r[0] >= hi))
        cur = lo
        for (a, b) in ivs:
            if a > cur:
                recs.append([cur, a, None, [op]])
            cur = max(cur, b)
        if cur < hi:
            recs.append([cur, hi, None, [op]])

    def emit(self):
        nc = self.nc
        counts = {}
        sems = {}
        n_dma = {"sp": 0, "pool": 0, "act": 0}
        for op in self.ops:
            for d in op.deps:
                if not d.dma and not (d.eng == "pe" and op.eng == "pe" and not op.dma):
                    d.signal = True
        held = []

        def new_sem(name):
            cm = nc.semaphore(name)
            s = cm.__enter__()
            held.append(cm)
            return s

        dma_sems = {}
        for op in self.ops:
            if op.dma:
                i = n_dma[op.eng]
                n_dma[op.eng] += 1
                key = (op.eng, i % self.NDMA_SEM)
                if key not in dma_sems:
                    dma_sems[key] = new_sem("d_%s_%d" % key)
                op.sem = dma_sems[key]
                op.val = 16 * (i // self.NDMA_SEM + 1)
            elif op.signal:
                c = counts.get(op.eng, 0)
                gen = c // self.SEM_ROT
                key = (op.eng, gen)
                if key not in sems:
                    sems[key] = new_sem("c_%s_%d" % key)
                op.sem = sems[key]
                op.val = c % self.SEM_ROT + 1
                counts[op.eng] = c + 1
        seen = {e: {} for e in self.eng}
        for op in self.ops:
            e = self.eng[op.eng]
            sn = seen[op.eng]
            for d in sorted(op.deps, key=lambda o: o.idx):
                if d.eng == "pe" and op.eng == "pe" and not d.dma and not op.dma:
                    continue
                k = d.sem.num
                if sn.get(k, 0) >= d.val:
                    continue
                e.wait_ge(d.sem, d.val)
                sn[k] = d.val
            ins = op.fn() if op.fn is not None else None
            if op.dma:
                ins.then_inc(op.sem, 16)
            elif op.signal:
                ins.then_inc(op.sem, 1)
        final = {}
        for op in self.ops:
            if op.dma:
                final[op.sem.num] = (op.sem, max(op.val, final.get(op.sem.num, (None, 0))[1]))
        for (sem, val) in final.values():
            nc.sync.wait_ge(sem, val)
        return held


class Buf:
    def __init__(self, ap, space, base, esz):
        self.ap = ap
        self.space = space
        self.base = base
        self.esz = esz

    def iv(self, c0, c1):
        return (self.space, self.base + c0 * self.esz, self.base + c1 * self.esz)


def build_program(NSEQ, NT, do_attn=True, do_ffn0=True, do_conv=True, do_ffn1=True):
    T = NMETA + 512 * NT
    NKT = 1 + 4 * NT
    TT = [(0, NMETA)] + [(NMETA + 512 * i, 512) for i in range(NT)]
    KT = [(0, NMETA)] + [(NMETA + 128 * j, 128) for j in range(4 * NT)]
    FFN_GROUPS = [[TT[0], TT[1]]] + [[t] for t in TT[2:]]
    GT = NMETA + 512

    nc = bass.Bass("TRN2", target_bir_lowering=False)
    P = Prog(nc)

    def din(name, shape):
        return nc.dram_tensor(name, list(shape), F32, kind="ExternalInput").ap()

    x_d = din("x", [NSEQ, 512 * NT, D])
    meta_d = din("meta_tokens", [NMETA, D])
    gm_d = din("mixer_norm_g", [2, D])
    gf_d = din("ffn_norm_g", [2, D])
    wqkv_d = din("attn_w_qkv", [D, 3 * D])
    gq_d = din("attn_q_gain", [1, 64])
    gk_d = din("attn_k_gain", [1, 64])
    lq1_d = din("attn_lambda_q1", [1, 64])
    lk1_d = din("attn_lambda_k1", [1, 64])
    lq2_d = din("attn_lambda_q2", [1, 64])
    lk2_d = din("attn_lambda_k2", [1, 64])
    gsub_d = din("attn_sub_gain", [1, 128])
    wo_d = din("attn_w_o", [D, D])
    win_d = din("conv_w_in", [D, 3 * D])
    cw_d = din("conv_w", [3, D])
    wout_d = din("conv_w_out", [D, D])
    wgu_d = din("ffn_w_gate_up", [2, D, 2 * DFF])
    wd_d = din("ffn_w_down", [2, DFF, D])
    ident_d = din("c_ident", [128, 128])
    ones_d = din("c_ones", [128, 128])
    bones_d = din("c_blockones", [128, 128])
    rperm_d = din("c_rperm", [128, 128])
    tri_d = din("c_tri", [128, 128])
    cos_d = din("c_cos", [128, T])
    sin_d = din("c_sin", [128, T])
    out_d = nc.dram_tensor("out", [NSEQ, 512 * NT, D], F32, kind="ExternalOutput").ap()

    def dscr(name, shape):
        return nc.dram_tensor(name, list(shape), BF16, kind="Internal").ap()

    wqkv_b = dscr("wqkv_b", [D, 3 * D])
    wo_b = dscr("wo_b", [D, D])
    win_b = dscr("win_b", [D, 3 * D])
    wout_b = dscr("wout_b", [D, D])
    wgu_b = dscr("wgu_b", [2, D, 2 * DFF])
    wd_b = dscr("wd_b", [2, DFF, D])

    h_t = nc.alloc_sbuf_tensor("h", [128, KC, T], F32)
    xn_t = nc.alloc_sbuf_tensor("xn", [128, KC, T], BF16)
    UBYTES = max(KC * T * 2 + 4 * (T + 2) + 8, 24576 + 16384)
    UBYTES = (UBYTES + 63) // 64 * 64
    TMPBYTES = 20480
    U_t = nc.alloc_sbuf_tensor("U", [128, UBYTES // 2], BF16)
    TMP_t = nc.alloc_sbuf_tensor("TMP", [128, TMPBYTES // 2], BF16)
    W3_t = [nc.alloc_sbuf_tensor("W3_%d" % i, [128, KC, 3, 128], BF16) for i in range(2)]
    W2_t = [nc.alloc_sbuf_tensor("W2_%d" % i, [128, KC, 2, 128], BF16) for i in range(2)]
    W1_t = [nc.alloc_sbuf_tensor("W1_%d" % i, [128, FC, 128], BF16) for i in range(2)]
    cos_t = nc.alloc_sbuf_tensor("cos", [128, T], F32)
    sin_t = nc.alloc_sbuf_tensor("sin", [128, T], F32)
    ident_t = nc.alloc_sbuf_tensor("ident", [128, 128], F32)
    identb_t = nc.alloc_sbuf_tensor("identb", [128, 128], BF16)
    ones_t = nc.alloc_sbuf_tensor("ones", [128, 128], BF16)
    bones_t = nc.alloc_sbuf_tensor("bones", [128, 128], BF16)
    rperm_t = nc.alloc_sbuf_tensor("rperm", [128, 128], BF16)
    tri_t = nc.alloc_sbuf_tensor("tri", [128, 128], BF16)
    metaT_t = nc.alloc_sbuf_tensor("metaT", [128, KC, NMETA], F32)
    gm_t = nc.alloc_sbuf_tensor("gm", [128, 2, KC], F32)
    gf_t = nc.alloc_sbuf_tensor("gf", [128, 2, KC], F32)
    cw_t = nc.alloc_sbuf_tensor("cw", [128, 3, KC], F32)
    sm_t = nc.alloc_sbuf_tensor("sm", [128, 16], F32)
    ps_t = nc.alloc_psum_tensor("ps", [128, 8, 512], F32)

    def sv(tname, lo, hi):
        return (tname, lo, hi)

    def whole(tname):
        return (tname, 0, 1 << 30)

    def ivh(c, t0, n):
        return ("h", (c * T + t0) * 4, (c * T + t0 + n) * 4)

    def ivxn(c, t0, n):
        return ("xn", (c * T + t0) * 2, (c * T + t0 + n) * 2)

    def ivps(b):
        return ("ps", b, b + 1)

    def uview(off, ncols, dt):
        esz = 4 if dt == F32 else 2
        assert off % 4 == 0 and off + ncols * esz <= UBYTES, (off, ncols, esz, UBYTES)
        a = U_t[:, off // 2: off // 2 + ncols * esz // 2]
        if dt == F32:
            a = a.bitcast(F32)
        return Buf(a, "U", off, esz)

    def tview(off, ncols, dt):
        esz = 4 if dt == F32 else 2
        assert off % 4 == 0 and off + ncols * esz <= TMPBYTES, (off, ncols, esz)
        a = TMP_t[:, off // 2: off // 2 + ncols * esz // 2]
        if dt == F32:
            a = a.bitcast(F32)
        return Buf(a, "TMP", off, esz)

    qT = uview(0, T, BF16)
    kT = uview(2 * T, T, BF16)
    vT = uview(4 * T, T, BF16)
    Vtm = uview(6 * T, NKT * 128, BF16)
    kT1 = uview(6 * T + NKT * 256, T, BF16)
    act = uview(0, FC * GT, BF16)
    yb = uview(0, KC * T, BF16)
    zb = uview(KC * T * 2, T + 2, F32)
    xs = [uview(24576 + 4096 * i, D, F32) for i in range(2)]
    osb = [uview(24576 + 8192 + 4096 * i, D, F32) for i in range(2)]
    n_sqb = tview(0, KC * 512, BF16)
    n_lnv = tview(8192, 512, F32)
    n_rstd = [tview(10240 + 2048 * i, 512, F32) for i in range(2)]
    r_sq = [tview(0 + 1024 * i, 512, BF16) for i in range(2)]
    r_rawb = [tview(2048 + 1024 * i, 512, BF16) for i in range(2)]
    r_t1 = [tview(4096 + 2048 * i, 512, F32) for i in range(2)]
    r_t2 = [tview(8192 + 2048 * i, 512, F32) for i in range(2)]
    r_lnv = tview(12288, 512, F32)
    r_rstd = [tview(14336 + 2048 * i, 512, F32) for i in range(2)]
    a_pT = [tview(1024 * i, 512, BF16) for i in range(4)]
    a_r = [tview(4096 + 2048 * i, 512, F32) for i in range(2)]
    a_a = [tview(8192 + 2048 * i, 512, F32) for i in range(2)]
    a_o = tview(12288, 512, F32)
    a_lnv = tview(14336, 512, F32)
    a_rstd = tview(16384, 512, F32)
    a_sq = tview(18432, 512, BF16)
    a_ao = tview(19456, 512, BF16)
    f_sg = [tview(2048 * i, 512, F32) for i in range(2)]
    c_cs = [tview(2048 * i, 512, F32) for i in range(2)]
    c_tmp = [tview(4096 + 2048 * i, 512, F32) for i in range(2)]

    lamb = tview(0, 256, F32)
    lam_t = lamb.ap.rearrange("p (i d) -> p i d", i=4)
    h3 = h_t[:, :, :]
    xn3 = xn_t[:, :, :]
    ps = ps_t[:, :, :]

    bank_ctr = {"A": 0, "B": 0, "ALL": 0, "S": 0, "X": 0}

    def bank(pool):
        i = bank_ctr[pool]
        bank_ctr[pool] = i + 1
        if pool == "A":
            return i % 4
        if pool == "B":
            return 4 + i % 4
        if pool == "S":
            return 4 + i % 3
        if pool == "X":
            return 7
        return i % 8

    def dma(eng, out_ap, in_ap, reads, writes, slow=False):
        e = P.eng[eng]
        if slow:
            fn = lambda: e.dma_start(out=out_ap, in_=in_ap, allow_slow_non_contiguous=True)
        else:
            fn = lambda: e.dma_start(out=out_ap, in_=in_ap)
        return P.add(eng, fn, reads=reads, writes=writes, dma=True)

    def mm_group(out_ap, out_iv, pairs, reads, start=True, stop=True):
        n = len(pairs)

        def fn():
            ins = None
            for i, (l, r) in enumerate(pairs):
                ins = nc.tensor.matmul(out_ap, l, r, start=(start and i == 0),
                                       stop=(stop and i == n - 1))
            return ins
        rd = list(reads)
        if not start:
            rd.append(out_iv)
        return P.add("pe", fn, reads=rd, writes=[out_iv])

    def act_op(out_ap, in_ap, func, reads, writes, bias=None, scale=1.0):
        if bias is None:
            fn = lambda: nc.scalar.activation(out=out_ap, in_=in_ap, func=func, scale=scale)
        else:
            fn = lambda: nc.scalar.activation(out=out_ap, in_=in_ap, func=func, bias=bias,
                                              scale=scale)
        return P.add("act", fn, reads=reads, writes=writes)

    def stt(eng, out_ap, in0, scalar, in1, op0, op1, reads, writes):
        e = P.eng[eng]
        return P.add(eng, lambda: e.scalar_tensor_tensor(out=out_ap, in0=in0, scalar=scalar,
                                                         in1=in1, op0=op0, op1=op1),
                     reads=reads, writes=writes)

    def tt(eng, out_ap, in0, in1, op, reads, writes):
        e = P.eng[eng]
        return P.add(eng, lambda: e.tensor_tensor(out=out_ap, in0=in0, in1=in1, op=op),
                     reads=reads, writes=writes)

    def tcopy(eng, out_ap, in_ap, reads, writes):
        if eng == "act":
            return P.add("act", lambda: nc.scalar.copy(out=out_ap, in_=in_ap), reads=reads,
                         writes=writes)
        e = P.eng[eng]
        return P.add(eng, lambda: e.tensor_copy(out_ap, in_ap), reads=reads, writes=writes)

    SM = "sm"

    def smc(i):
        return sm_t[:, i:i + 1]

    def cast_dma(out_ap, in_ap, reads, writes):
        return dma("pool", out_ap, in_ap, reads, writes)

    def cast_rows(dst, src, name, nrows, step):
        for r0 in range(0, nrows, step):
            r1 = min(nrows, r0 + step)
            cast_dma(dst[r0:r1, :], src[r0:r1, :], [], [(name, r0, r1)])

    cast_rows(wqkv_b, wqkv_d, "wqkv_b", D, 256)
    cast_rows(wo_b, wo_d, "wo_b", D, 512)
    for l in range(2):
        cast_rows(wgu_b[l], wgu_d[l], "wgu_b%d" % l, D, 128)
        cast_rows(wd_b[l], wd_d[l], "wd_b%d" % l, DFF, 704)
        if l == 0:
            cast_rows(win_b, win_d, "win_b", D, 256)
            cast_rows(wout_b, wout_d, "wout_b", D, 512)

    dma("sp", ident_t[:, :], ident_d[:, :], [], [whole("ident")])
    cast_dma(identb_t[:, :], ident_d[:, :], [], [whole("identb")])
    cast_dma(ones_t[:, :], ones_d[:, :], [], [whole("ones")])
    cast_dma(bones_t[:, :], bones_d[:, :], [], [whole("bones")])
    cast_dma(rperm_t[:, :], rperm_d[:, :], [], [whole("rperm")])
    cast_dma(tri_t[:, :], tri_d[:, :], [], [whole("tri")])
    dma("sp", cos_t[:, :], cos_d[:, :], [], [whole("cos")])
    dma("sp", sin_t[:, :], sin_d[:, :], [], [whole("sin")])
    for l in range(2):
        dma("sp", gm_t[:, l, :], gm_d[l, :].rearrange("(c p) -> p c", p=128), [],
            [("gm", l, l + 1)], slow=True)
        dma("sp", gf_t[:, l, :], gf_d[l, :].rearrange("(c p) -> p c", p=128), [],
            [("gf", l, l + 1)], slow=True)
    for w in range(3):
        dma("sp", cw_t[:, w, :], cw_d[w, :].rearrange("(c p) -> p c", p=128), [],
            [("cw", w, w + 1)], slow=True)

    def col64(src, lo):
        return src[0, lo:lo + 32].rearrange("(d o) -> d o", o=1)

    for (col, src) in ((0, gq_d), (2, gk_d)):
        for half in range(2):
            for q in range(2):
                p0 = half * 64 + q * 32
                dma("sp", sm_t[p0:p0 + 32, col:col + 1], col64(src, q * 32), [],
                    [(SM, col * 1000 + p0, col * 1000 + p0 + 32)], slow=True)
                dma("sp", sm_t[p0:p0 + 32, col + 1:col + 2], col64(src, (1 - q) * 32), [],
                    [(SM, (col + 1) * 1000 + p0, (col + 1) * 1000 + p0 + 32)], slow=True)
    dma("sp", sm_t[:, 4:5], gsub_d[0, :].rearrange("(d o) -> d o", o=1), [],
        [(SM, 4000, 4128)], slow=True)
    for i, src in enumerate((lq1_d, lk1_d, lq2_d, lk2_d)):
        dma("sp", lam_t[:, i, :], src[0:1, :].partition_broadcast(128), [],
            [lamb.iv(i * 64, (i + 1) * 64)])
    P.add("dve", lambda: nc.vector.memset(sm_t[:, 5:6], EPS), writes=[(SM, 5000, 5128)])
    P.add("dve", lambda: nc.vector.tensor_scalar(out=sm_t[:, 0:2], in0=sm_t[:, 0:2],
                                                 scalar1=0.125, scalar2=None, op0=ALU.mult),
          reads=[(SM, 0, 2000)], writes=[(SM, 0, 2000)])
    P.add("dve", lambda: nc.vector.tensor_scalar(out=sm_t[:, 4:5], in0=sm_t[:, 4:5],
                                                 scalar1=1.0 - LAMBDA_INIT0, scalar2=None,
                                                 op0=ALU.mult),
          reads=[(SM, 4000, 4128)], writes=[(SM, 4000, 4128)])
    for i in range(2):
        P.add("dve", lambda i=i: nc.vector.tensor_tensor(out=lam_t[:, 2 * i, :],
                                                         in0=lam_t[:, 2 * i, :],
                                                         in1=lam_t[:, 2 * i + 1, :],
                                                         op=ALU.mult),
              reads=[lamb.iv(2 * i * 64, (2 * i + 2) * 64)], writes=[lamb.iv(2 * i * 64, (2 * i + 1) * 64)])
        P.add("dve", lambda i=i: nc.vector.tensor_reduce(out=sm_t[:, 9 + i:10 + i],
                                                         in_=lam_t[:, 2 * i, :], axis=AX.X,
                                                         op=ALU.add),
              reads=[lamb.iv(2 * i * 64, (2 * i + 1) * 64)], writes=[(SM, (9 + i) * 1000, (9 + i) * 1000 + 128)])
        act_op(sm_t[:, 7 + i:8 + i], sm_t[:, 9 + i:10 + i], AF.Exp,
               [(SM, (9 + i) * 1000, (9 + i) * 1000 + 128)],
               [(SM, (7 + i) * 1000, (7 + i) * 1000 + 128)])
    P.add("dve", lambda: nc.vector.tensor_tensor(out=sm_t[:, 6:7], in0=sm_t[:, 8:9],
                                                 in1=sm_t[:, 7:8], op=ALU.subtract),
          reads=[(SM, 7000, 8128)], writes=[(SM, 6000, 6128)])
    P.add("dve", lambda: nc.vector.tensor_scalar(out=sm_t[:, 6:7], in0=sm_t[:, 6:7],
                                                 scalar1=-LAMBDA_INIT0, scalar2=None,
                                                 op0=ALU.add),
          reads=[(SM, 6000, 6128)], writes=[(SM, 6000, 6128)])

    dma("sp", xs[0].ap[0:NMETA, :], meta_d[:, :], [], [xs[0].iv(0, D)])
    for half in range(2):
        b = bank("ALL")
        c0 = half * 4

        def fn(b=b, c0=c0):
            ins = None
            for cc in range(4):
                ins = nc.tensor.transpose(ps[:, b, cc * 128: cc * 128 + NMETA],
                                          xs[0].ap[0:NMETA, (c0 + cc) * 128:(c0 + cc + 1) * 128],
                                          ident_t[0:NMETA, 0:NMETA])
            return ins
        P.add("pe", fn, reads=[xs[0].iv(0, D), whole("ident")], writes=[ivps(b)])
        tcopy("dve", metaT_t[:, c0:c0 + 4, :],
              ps[:, b, :].rearrange("p (c n) -> p c n", c=4)[:, :, 0:NMETA],
              [ivps(b)], [("metaT", c0, c0 + 4)])

    wslot = {"W3": 0, "W2": 0, "W1": 0, "WO": 0}

    def next_slot(kind):
        i = wslot[kind]
        wslot[kind] = i + 1
        return i % 2

    def rmsnorm(g_t, l, tiles):
        for ti, (t0, n) in enumerate(tiles):
            sq3 = n_sqb.ap.rearrange("p (c n) -> p c n", c=KC)[:, :, 0:n]
            act_op(sq3, h3[:, :, t0:t0 + n], AF.Square,
                   [ivh(c, t0, n) for c in range(KC)], [n_sqb.iv(0, KC * 512)])
            b = bank("B")
            mm_group(ps[:, b, 0:n], ivps(b),
                     [(ones_t[:, :], sq3[:, c, :]) for c in range(KC)],
                     [whole("ones"), n_sqb.iv(0, KC * 512)])
            act_op(n_lnv.ap[:, 0:n], ps[:, b, 0:n], AF.Ln, [ivps(b), (SM, 5000, 5128)],
                   [n_lnv.iv(0, n)], bias=smc(5), scale=1.0 / D)
            rs = n_rstd[ti % 2]
            act_op(rs.ap[:, 0:n], n_lnv.ap[:, 0:n], AF.Exp, [n_lnv.iv(0, n)], [rs.iv(0, n)],
                   scale=-0.5)
            for c in range(KC):
                stt("dve", xn3[:, c, t0:t0 + n], h3[:, c, t0:t0 + n], g_t[:, l, c:c + 1],
                    rs.ap[:, 0:n], ALU.mult, ALU.mult,
                    [ivh(c, t0, n), rs.iv(0, n), whole("gm"), whole("gf")], [ivxn(c, t0, n)])

    def load_w3(src_b, srcname, blk):
        s = next_slot("W3")
        v = src_b.rearrange("(kc p) n -> p kc n", p=128)
        for which in range(3):
            c0 = which * D + blk * 128
            dma("sp", W3_t[s][:, :, which, :], v[:, :, c0:c0 + 128], [whole(srcname)],
                [("W3_%d" % s, which, which + 1)])
        return s

    def ffn(l, tiles_groups):
        gname = "wgu_b%d" % l
        dname = "wd_b%d" % l
        gu_v = wgu_b[l].rearrange("(kc p) n -> p kc n", p=128)
        wd_v = wd_b[l].rearrange("(j p) n -> p j n", p=128)
        for grp in tiles_groups:
            rmsnorm(gf_t, l, grp)
            offs = []
            o = 0
            for (t0, n) in grp:
                offs.append(o)
                o += n
            act3 = act.ap.rearrange("p (j t) -> p j t", j=FC)
            pending = None

            def load_gu(j):
                s = next_slot("W2")
                dma("sp", W2_t[s][:, :, 0, :], gu_v[:, :, j * 128:(j + 1) * 128],
                    [whole(gname)], [("W2_%d" % s, 0, 1)])
                dma("sp", W2_t[s][:, :, 1, :], gu_v[:, :, DFF + j * 128: DFF + (j + 1) * 128],
                    [whole(gname)], [("W2_%d" % s, 1, 2)])
                return s

            def load_d(c):
                s = next_slot("W1")
                dma("sp", W1_t[s][:, :, :], wd_v[:, :, c * 128:(c + 1) * 128], [whole(dname)],
                    [whole("W1_%d" % s)])
                return s

            slots = [load_gu(0)]
            for j in range(FC):
                if j + 1 < FC:
                    slots.append(load_gu(j + 1))
                s = slots[j]
                for gi, (t0, n) in enumerate(grp):
                    bg = bank("ALL")
                    bu = bank("ALL")
                    rhs_reads = [ivxn(c, t0, n) for c in range(KC)]
                    mm_group(ps[:, bg, 0:n], ivps(bg),
                             [(W2_t[s][:, kc, 0, :], xn3[:, kc, t0:t0 + n]) for kc in range(KC)],
                             rhs_reads + [("W2_%d" % s, 0, 1)])
                    mm_group(ps[:, bu, 0:n], ivps(bu),
                             [(W2_t[s][:, kc, 1, :], xn3[:, kc, t0:t0 + n]) for kc in range(KC)],
                             rhs_reads + [("W2_%d" % s, 1, 2)])
                    sg = f_sg[(j * len(grp) + gi) % 2]
                    act_op(sg.ap[:, 0:n], ps[:, bg, 0:n], AF.Silu, [ivps(bg)], [sg.iv(0, n)])
                    o0 = j * GT + offs[gi]
                    tt("dve", act3[:, j, offs[gi]:offs[gi] + n], sg.ap[:, 0:n], ps[:, bu, 0:n],
                       ALU.mult, [sg.iv(0, n), ivps(bu)], [act.iv(o0, o0 + n)])
            dslots = [load_d(0)]
            for c in range(KC):
                if c + 1 < KC:
                    dslots.append(load_d(c + 1))
                s = dslots[c]
                for gi, (t0, n) in enumerate(grp):
                    b = bank("ALL")
                    mm_group(ps[:, b, 0:n], ivps(b),
                             [(W1_t[s][:, j, :], act3[:, j, offs[gi]:offs[gi] + n])
                              for j in range(FC)],
                             [whole("W1_%d" % s)] +
                             [act.iv(j * GT + offs[gi], j * GT + offs[gi] + n) for j in range(FC)])
                    tt("dve", h3[:, c, t0:t0 + n], h3[:, c, t0:t0 + n], ps[:, b, 0:n], ALU.add,
                       [ivh(c, t0, n), ivps(b)], [ivh(c, t0, n)])

    def attention():
        P.add("dve", lambda: nc.vector.memset(kT.ap[64:128, :], 0.0), writes=[kT.iv(0, T)])
        P.add("dve", lambda: nc.vector.memset(kT1.ap[0:64, :], 0.0), writes=[kT1.iv(0, T)])
        rmsnorm(gm_t, 0, TT)
        wo_v = wo_b
        nxt = load_w3(wqkv_b, "wqkv_b", 0)
        for hd in range(H):
            s3 = nxt
            so = next_slot("W1")
            dma("sp", W1_t[so][:, 0:KC, :],
                wo_v[hd * 128:(hd + 1) * 128, :].rearrange("p (c n) -> p c n", c=KC),
                [whole("wo_b")], [whole("W1_%d" % so)])
            if hd + 1 < H:
                nxt = load_w3(wqkv_b, "wqkv_b", hd + 1)
            it = 0
            pendB = []
            for (t0, n) in TT:
                rhs_reads = [ivxn(c, t0, n) for c in range(KC)]
                for which in (2, 0, 1):
                    bp = bank("ALL")
                    mm_group(ps[:, bp, 0:n], ivps(bp),
                             [(W3_t[s3][:, kc, which, :], xn3[:, kc, t0:t0 + n])
                              for kc in range(KC)],
                             rhs_reads + [("W3_%d" % s3, which, which + 1)])
                    if which == 2:
                        tcopy("act", vT.ap[:, t0:t0 + n], ps[:, bp, 0:n], [ivps(bp)],
                              [vT.iv(t0, t0 + n)])
                        if pendB:
                            pendB.pop(0)()
                        continue
                    k2 = it % 2
                    it += 1
                    gcol = 0 if which == 0 else 2
                    sq, rawb, t1, t2, rs = r_sq[k2], r_rawb[k2], r_t1[k2], r_t2[k2], r_rstd[k2]
                    act_op(sq.ap[:, 0:n], ps[:, bp, 0:n], AF.Square, [ivps(bp)], [sq.iv(0, n)])
                    act_op(t1.ap[:, 0:n], ps[:, bp, 0:n], AF.Copy, [ivps(bp), (SM, 0, 4000)],
                           [t1.iv(0, n)], scale=smc(gcol))
                    tcopy("dve", rawb.ap[:, 0:n], t1.ap[:, 0:n], [t1.iv(0, n)], [rawb.iv(0, n)])
                    tt("dve", t1.ap[:, 0:n], t1.ap[:, 0:n], cos_t[:, t0:t0 + n], ALU.mult,
                       [t1.iv(0, n), whole("cos")], [t1.iv(0, n)])
                    if pendB:
                        pendB.pop(0)()

                    def stageB(which=which, t0=t0, n=n, sq=sq, rawb=rawb, t1=t1, t2=t2, rs=rs):
                        bs = bank("ALL")
                        mm_group(ps[:, bs, 0:n], ivps(bs), [(bones_t[:, :], sq.ap[:, 0:n])],
                                 [whole("bones"), sq.iv(0, n)])
                        br = bank("ALL")
                        mm_group(ps[:, br, 0:n], ivps(br), [(rperm_t[:, :], rawb.ap[:, 0:n])],
                                 [whole("rperm"), rawb.iv(0, n)])
                        act_op(r_lnv.ap[:, 0:n], ps[:, bs, 0:n], AF.Ln,
                               [ivps(bs), (SM, 5000, 5128)], [r_lnv.iv(0, n)], bias=smc(5),
                               scale=1.0 / 64)
                        act_op(rs.ap[:, 0:n], r_lnv.ap[:, 0:n], AF.Exp, [r_lnv.iv(0, n)],
                               [rs.iv(0, n)], scale=-0.5)
                        tt("dve", t2.ap[:, 0:n], sin_t[:, t0:t0 + n], ps[:, br, 0:n], ALU.mult,
                           [ivps(br), whole("sin")], [t2.iv(0, n)])
                        tt("dve", t1.ap[:, 0:n], t1.ap[:, 0:n], t2.ap[:, 0:n], ALU.add,
                           [t1.iv(0, n), t2.iv(0, n)], [t1.iv(0, n)])
                        if which == 0:
                            tt("dve", qT.ap[:, t0:t0 + n], t1.ap[:, 0:n], rs.ap[:, 0:n],
                               ALU.mult, [t1.iv(0, n), rs.iv(0, n)], [qT.iv(t0, t0 + n)])
                        else:
                            tt("dve", kT.ap[0:64, t0:t0 + n], t1.ap[0:64, 0:n],
                               rs.ap[0:64, 0:n], ALU.mult, [t1.iv(0, n), rs.iv(0, n)],
                               [kT.iv(t0, t0 + n)])
                            tt("dve", kT1.ap[64:128, t0:t0 + n], t1.ap[64:128, 0:n],
                               rs.ap[64:128, 0:n], ALU.mult, [t1.iv(0, n), rs.iv(0, n)],
                               [kT1.iv(t0, t0 + n)])
                    pendB.append(stageB)
            while pendB:
                pendB.pop(0)()
            if DEBUG_ATTN_STAGE < 2:
                continue
            vt3 = Vtm.ap.rearrange("p (j d) -> p j d", d=128)
            b = bank("ALL")
            psb = ps[:, b, 0:256].bitcast(BF16)
            P.add("pe", lambda psb=psb: nc.tensor.transpose(psb[0:NMETA, 0:128],
                                                            vT.ap[:, 0:NMETA], identb_t[:, :]),
                  reads=[vT.iv(0, NMETA), whole("identb")], writes=[ivps(b)])
            tcopy("dve", vt3[0:NMETA, 0, :], psb[0:NMETA, 0:128], [ivps(b)], [Vtm.iv(0, 128)])
            for j0 in range(1, NKT, 4):
                b = bank("ALL")
                psb = ps[:, b, 0:256].bitcast(BF16)

                def fn(psb=psb, j0=j0):
                    ins = None
                    for jj in range(4):
                        k0 = KT[j0 + jj][0]
                        ins = nc.tensor.transpose(psb[:, jj * 128:(jj + 1) * 128],
                                                  vT.ap[:, k0:k0 + 128], identb_t[:, :])
                    return ins
                kk0 = KT[j0][0]
                P.add("pe", fn, reads=[vT.iv(kk0, kk0 + 512), whole("identb")], writes=[ivps(b)])
                tcopy("dve" if (j0 // 4) % 2 == 0 else "act", vt3[:, j0:j0 + 4, :],
                      psb.rearrange("p (j d) -> p j d", d=128), [ivps(b)],
                      [Vtm.iv(j0 * 128, (j0 + 4) * 128)])
            pend = []

            def flush_some(k):
                for _ in range(k):
                    if pend:
                        pend.pop(0)()
            for (t0, n) in TT:
                if DEBUG_ATTN_STAGE < 3:
                    continue
                bo = [bank("A"), bank("A")]
                bsum = [bank("A"), bank("A")]
                items = []
                for j, (k0, kn) in enumerate(KT):
                    if k0 >= t0 + n:
                        break
                    diag = k0 >= t0
                    qs = (k0 - t0) if diag else 0
                    for m in range(2):
                        items.append((j, k0, kn, diag, qs, m))

                def emit_st(item):
                    j, k0, kn, diag, qs, m = item
                    nq = n - qs
                    bst = bank("S")
                    mm_group(ps[0:kn, bst, 0:nq], ivps(bst),
                             [((kT if m == 0 else kT1).ap[:, k0:k0 + kn],
                               qT.ap[:, t0 + qs:t0 + n])],
                             [(kT if m == 0 else kT1).iv(k0, k0 + kn), qT.iv(t0 + qs, t0 + n)])
                    return bst

                def emit_av(item, bst, idx):
                    j, k0, kn, diag, qs, m = item
                    nq = n - qs
                    pT = a_pT[idx % 4]
                    act_op(pT.ap[0:kn, 0:nq], ps[0:kn, bst, 0:nq], AF.Exp, [ivps(bst)],
                           [pT.iv(0, nq)])
                    if diag:
                        tt("dve", pT.ap[0:kn, 0:kn], pT.ap[0:kn, 0:kn], tri_t[0:kn, 0:kn],
                           ALU.mult, [pT.iv(0, kn), whole("tri")], [pT.iv(0, kn)])
                    first = (j == 0)
                    last = (j == items[-1][0])
                    mm_group(ps[:, bo[m], qs:n], ivps(bo[m]),
                             [(vt3[0:kn, j, :], pT.ap[0:kn, 0:nq])],
                             [Vtm.iv(j * 128, (j + 1) * 128), pT.iv(0, nq)], start=first,
                             stop=last)
                    mm_group(ps[:, bsum[m], qs:n], ivps(bsum[m]),
                             [(ones_t[0:kn, :], pT.ap[0:kn, 0:nq])],
                             [whole("ones"), pT.iv(0, nq)], start=first, stop=last)

                LOOK = 2
                sts = []
                for i in range(min(LOOK, len(items))):
                    sts.append(emit_st(items[i]))
                for i, item in enumerate(items):
                    if i + LOOK < len(items):
                        sts.append(emit_st(items[i + LOOK]))
                    emit_av(item, sts[i], i)
                    if i >= 1:
                        flush_some(1)
                flush_some(len(pend))
                if DEBUG_ATTN_STAGE < 4:
                    continue
                for m in range(2):
                    act_op(a_r[m].ap[:, 0:n], ps[:, bsum[m], 0:n], AF.Ln, [ivps(bsum[m])],
                           [a_r[m].iv(0, n)])
                    act_op(a_r[m].ap[:, 0:n], a_r[m].ap[:, 0:n], AF.Exp, [a_r[m].iv(0, n)],
                           [a_r[m].iv(0, n)], scale=-1.0)
                    tt("dve", a_a[m].ap[:, 0:n], a_r[m].ap[:, 0:n], ps[:, bo[m], 0:n], ALU.mult,
                       [ivps(bo[m]), a_r[m].iv(0, n)], [a_a[m].iv(0, n)])
                stt("dve", a_o.ap[:, 0:n], a_a[1].ap[:, 0:n], smc(6), a_a[0].ap[:, 0:n],
                    ALU.mult, ALU.add, [a_a[0].iv(0, n), a_a[1].iv(0, n), (SM, 6000, 6128)],
                    [a_o.iv(0, n)])
                act_op(a_sq.ap[:, 0:n], a_o.ap[:, 0:n], AF.Square, [a_o.iv(0, n)],
                       [a_sq.iv(0, n)])

                def f1(t0=t0, n=n):
                    bq = bank("X")
                    mm_group(ps[:, bq, 0:n], ivps(bq), [(ones_t[:, :], a_sq.ap[:, 0:n])],
                             [whole("ones"), a_sq.iv(0, n)])
                    act_op(a_lnv.ap[:, 0:n], ps[:, bq, 0:n], AF.Ln, [ivps(bq), (SM, 5000, 5128)],
                           [a_lnv.iv(0, n)], bias=smc(5), scale=1.0 / 128)
                    act_op(a_rstd.ap[:, 0:n], a_lnv.ap[:, 0:n], AF.Exp, [a_lnv.iv(0, n)],
                           [a_rstd.iv(0, n)], scale=-0.5)
                    stt("dve", a_ao.ap[:, 0:n], a_o.ap[:, 0:n], smc(4), a_rstd.ap[:, 0:n],
                        ALU.mult, ALU.mult, [a_o.iv(0, n), a_rstd.iv(0, n), (SM, 4000, 4128)],
                        [a_ao.iv(0, n)])
                pend.append(f1)
                for c in range(KC):
                    def f2(c=c, t0=t0, n=n, so=so):
                        b = bank("X")
                        mm_group(ps[:, b, 0:n], ivps(b), [(W1_t[so][:, c, :], a_ao.ap[:, 0:n])],
                                 [whole("W1_%d" % so), a_ao.iv(0, n)])
                        tt("dve", h3[:, c, t0:t0 + n], h3[:, c, t0:t0 + n], ps[:, b, 0:n],
                           ALU.add, [ivh(c, t0, n), ivps(b)], [ivh(c, t0, n)])
                    pend.append(f2)
            flush_some(len(pend))

    def conv():
        rmsnorm(gm_t, 1, TT)
        yb3 = yb.ap.rearrange("p (c t) -> p c t", c=KC)
        nxt = load_w3(win_b, "win_b", 0)
        P.add("pool", lambda: nc.gpsimd.memset(zb.ap[:, 0:2], 0.0), writes=[zb.iv(0, 2)])
        it = 0
        for c in range(KC):
            s3 = nxt
            if c + 1 < KC:
                nxt = load_w3(win_b, "win_b", c + 1)
            for (t0, n) in TT:
                rhs_reads = [ivxn(cc, t0, n) for cc in range(KC)]
                bks = []
                for which in range(3):
                    bp = bank("ALL")
                    bks.append(bp)
                    mm_group(ps[:, bp, 0:n], ivps(bp),
                             [(W3_t[s3][:, kc, which, :], xn3[:, kc, t0:t0 + n])
                              for kc in range(KC)],
                             rhs_reads + [("W3_%d" % s3, which, which + 1)])
                k2 = it % 2
                it += 1
                cs, tmp = c_cs[k2], c_tmp[k2]
                tcopy("act", cs.ap[:, 0:n], ps[:, bks[1], 0:n], [ivps(bks[1])], [cs.iv(0, n)])
                tt("dve", zb.ap[:, 2 + t0:2 + t0 + n], cs.ap[:, 0:n], ps[:, bks[2], 0:n],
                   ALU.mult, [cs.iv(0, n), ivps(bks[2])], [zb.iv(2 + t0, 2 + t0 + n)])
                P.add("dve", lambda t0=t0, n=n, tmp=tmp, c=c: nc.vector.tensor_scalar(
                    out=tmp.ap[:, 0:n], in0=zb.ap[:, t0:t0 + n], scalar1=cw_t[:, 0, c:c + 1],
                    scalar2=None, op0=ALU.mult),
                    reads=[zb.iv(t0, t0 + n), whole("cw")], writes=[tmp.iv(0, n)])
                stt("dve", tmp.ap[:, 0:n], zb.ap[:, 1 + t0:1 + t0 + n], cw_t[:, 1, c:c + 1],
                    tmp.ap[:, 0:n], ALU.mult, ALU.add,
                    [zb.iv(1 + t0, 1 + t0 + n), tmp.iv(0, n), whole("cw")], [tmp.iv(0, n)])
                stt("dve", tmp.ap[:, 0:n], zb.ap[:, 2 + t0:2 + t0 + n], cw_t[:, 2, c:c + 1],
                    tmp.ap[:, 0:n], ALU.mult, ALU.add,
                    [zb.iv(2 + t0, 2 + t0 + n), tmp.iv(0, n), whole("cw")], [tmp.iv(0, n)])
                tt("dve", yb3[:, c, t0:t0 + n], tmp.ap[:, 0:n], ps[:, bks[0], 0:n], ALU.mult,
                   [tmp.iv(0, n), ivps(bks[0])], [yb.iv(c * T + t0, c * T + t0 + n)])
        wout_v = wout_b.rearrange("(kc p) n -> p kc n", p=128)

        def load_wout(co):
            s = next_slot("W1")
            dma("sp", W1_t[s][:, 0:KC, :], wout_v[:, :, co * 128:(co + 1) * 128],
                [whole("wout_b")], [whole("W1_%d" % s)])
            return s
        slots = [load_wout(0)]
        for co in range(KC):
            if co + 1 < KC:
                slots.append(load_wout(co + 1))
            s = slots[co]
            for (t0, n) in TT:
                b = bank("ALL")
                mm_group(ps[:, b, 0:n], ivps(b),
                         [(W1_t[s][:, c, :], yb3[:, c, t0:t0 + n]) for c in range(KC)],
                         [whole("W1_%d" % s)] +
                         [yb.iv(c * T + t0, c * T + t0 + n) for c in range(KC)])
                tt("dve", h3[:, co, t0:t0 + n], h3[:, co, t0:t0 + n], ps[:, b, 0:n], ALU.add,
                   [ivh(co, t0, n), ivps(b)], [ivh(co, t0, n)])

    def load_meta():
        tcopy("pool", h3[:, :, 0:NMETA], metaT_t[:, :, :], [whole("metaT")],
              [ivh(c, 0, NMETA) for c in range(KC)])

    def load_x(s, blks=None):
        if blks is None:
            load_meta()
            blks = range(4 * NT)
        for blk in blks:
            t0 = NMETA + blk * 128
            xb = xs[blk % 2]
            dma("sp", xb.ap[:, :], x_d[s, blk * 128:(blk + 1) * 128, :], [], [xb.iv(0, D)])
            for half in range(2):
                b = bank("ALL")
                c0 = half * 4

                def fn(b=b, c0=c0, xb=xb):
                    ins = None
                    for cc in range(4):
                        ins = nc.tensor.transpose(ps[:, b, cc * 128:(cc + 1) * 128],
                                                  xb.ap[:, (c0 + cc) * 128:(c0 + cc + 1) * 128],
                                                  ident_t[:, :])
                    return ins
                P.add("pe", fn, reads=[xb.iv(0, D), whole("ident")], writes=[ivps(b)])
                tcopy("act" if half == 0 else "dve", h3[:, c0:c0 + 4, t0:t0 + 128],
                      ps[:, b, :].rearrange("p (c n) -> p c n", c=4), [ivps(b)],
                      [ivh(c, t0, 128) for c in range(c0, c0 + 4)])

    def store_out(s, blks=None):
        if blks is None:
            blks = range(4 * NT)
        for blk in blks:
            t0 = NMETA + blk * 128
            ob = osb[blk % 2]
            for half in range(2):
                b = bank("ALL")
                c0 = half * 4

                def fn(b=b, c0=c0, t0=t0):
                    ins = None
                    for cc in range(4):
                        ins = nc.tensor.transpose(ps[:, b, cc * 128:(cc + 1) * 128],
                                                  h3[:, c0 + cc, t0:t0 + 128], ident_t[:, :])
                    return ins
                P.add("pe", fn, reads=[ivh(c, t0, 128) for c in range(c0, c0 + 4)] +
                      [whole("ident")], writes=[ivps(b)])
                tcopy("act" if half == 0 else "dve", ob.ap[:, c0 * 128:(c0 + 4) * 128],
                      ps[:, b, :], [ivps(b)], [ob.iv(c0 * 128, (c0 + 4) * 128)])
            dma("sp", out_d[s, blk * 128:(blk + 1) * 128, :], ob.ap[:, :], [ob.iv(0, D)],
                [("out", (s * 4 * NT + blk), (s * 4 * NT + blk) + 1)])

    for s in range(NSEQ):
        if s == 0:
            load_x(s)
        if do_attn:
            attention()
        if do_ffn0:
            ffn(0, FFN_GROUPS)
        if do_conv:
            conv()
        if do_ffn1:
            ffn(1, FFN_GROUPS)
        if s + 1 < NSEQ:
            for blk in range(4 * NT):
                store_out(s, [blk])
                load_x(s + 1, [blk])
            load_meta()
        else:
            store_out(s)
    P.add("sp", None, reads=[("out", 0, 1 << 30)])
    held = P.emit()
    return nc, held, len(P.ops)


def make_consts(T):
    ident = np.eye(128, dtype=np.float32)
    ones = np.ones((128, 128), np.float32)
    bones = np.zeros((128, 128), np.float32)
    bones[:64, :64] = 1.0
    bones[64:, 64:] = 1.0
    p = np.arange(128)
    partner = np.where((p % 64) < 32, p + 32, p - 32)
    rperm = np.zeros((128, 128), np.float32)
    rperm[partner, p] = 1.0
    kk = np.arange(128)[:, None]
    qq = np.arange(128)[None, :]
    tri = (qq >= kk).astype(np.float32)
    inv = (1.0 / (np.float32(10000.0) ** (np.arange(0, 64, 2, dtype=np.float32) / np.float32(64))))
    inv = inv.astype(np.float32)
    pos = np.arange(T, dtype=np.float32)
    ang = (pos[:, None] * inv[None, :]).astype(np.float32)
    cosv = np.cos(ang).astype(np.float32)
    sinv = np.sin(ang).astype(np.float32)
    f = p % 32
    cos_t = np.ascontiguousarray(cosv[:, f].T)
    sgn = np.where((p % 64) < 32, -1.0, 1.0).astype(np.float32)
    sin_t = np.ascontiguousarray((sinv[:, f] * sgn[None, :]).T)
    return dict(c_ident=ident, c_ones=ones, c_blockones=bones, c_rperm=rperm, c_tri=tri,
                c_cos=cos_t.astype(np.float32), c_sin=sin_t.astype(np.float32))


_CACHE = {}


def run(inputs, NSEQ, NT, n_cores=N_CORES, **flags):
    key = (NSEQ, NT, tuple(sorted(flags.items())))
    if key not in _CACHE:
        _CACHE[key] = build_program(NSEQ, NT, **flags)
    nc = _CACHE[key][0]
    T = NMETA + 512 * NT
    f32 = lambda a: np.ascontiguousarray(np.asarray(a, dtype=np.float32))
    shared = dict(
        meta_tokens=f32(inputs["meta_tokens"]),
        mixer_norm_g=f32(inputs["mixer_norm_g"]),
        ffn_norm_g=f32(inputs["ffn_norm_g"]),
        attn_w_qkv=f32(inputs["attn_w_qkv"][0]),
        attn_q_gain=f32(inputs["attn_q_gain"]),
        attn_k_gain=f32(inputs["attn_k_gain"]),
        attn_lambda_q1=f32(inputs["attn_lambda_q1"]),
        attn_lambda_k1=f32(inputs["attn_lambda_k1"]),
        attn_lambda_q2=f32(inputs["attn_lambda_q2"]),
        attn_lambda_k2=f32(inputs["attn_lambda_k2"]),
        attn_sub_gain=f32(inputs["attn_sub_gain"]),
        attn_w_o=f32(inputs["attn_w_o"][0]),
        conv_w_in=f32(inputs["conv_w_in"][0]),
        conv_w=f32(np.asarray(inputs["conv_w"]).reshape(3, D)),
        conv_w_out=f32(inputs["conv_w_out"][0]),
        ffn_w_gate_up=f32(inputs["ffn_w_gate_up"]),
        ffn_w_down=f32(inputs["ffn_w_down"]),
    )
    shared.update(make_consts(T))
    x = f32(inputs["x"])
    in_maps = []
    for c in range(n_cores):
        m = dict(shared)
        m["x"] = np.ascontiguousarray(x[c * NSEQ:(c + 1) * NSEQ])
        in_maps.append(m)
    res = run_bass_kernel_spmd(nc, in_maps, core_ids=list(range(n_cores)))
    return np.concatenate([np.asarray(r["out"]) for r in res.results], axis=0)


def kernel(**inputs):
    out = run(inputs, NSEQ=4, NT=4)
    return out.astype(np.float32)
```

```python
import math
import numpy as np
import concourse.bass as bass
import concourse.mybir as mybir
from concourse.bass_utils import run_bass_kernel_spmd

F32 = mybir.dt.float32
BF16 = mybir.dt.bfloat16
AF = mybir.ActivationFunctionType
ALU = mybir.AluOpType
AX = mybir.AxisListType

D = 1024
KC = 8
H = 8
DFF = 2816
FC = 22
NMETA = 16
EPS = 1e-6
LAMBDA_INIT0 = 0.8 - 0.6 * math.exp(-0.3 * 0)
N_CORES = 8
DEBUG_ATTN_STAGE = 9
DEBUG_ROPE = 9


class Op:
    __slots__ = ("eng", "fn", "deps", "dma", "signal", "sem", "val", "idx")


class Prog:
    SEM_ROT = 20000
    NDMA_SEM = 24

    def __init__(self, nc):
        self.nc = nc
        self.ops = []
        self.spaces = {}
        self.eng = {"pe": nc.tensor, "act": nc.scalar, "dve": nc.vector, "pool": nc.gpsimd,
                    "sp": nc.sync}
        self.dma_hist = {"sp": [], "pool": [], "act": []}

    def add(self, eng, fn, reads=(), writes=(), dma=False):
        op = Op()
        op.eng = eng
        op.fn = fn
        op.dma = dma
        op.signal = False
        op.sem = None
        op.val = 0
        op.idx = len(self.ops)
        op.deps = set()
        for (sp, lo, hi) in reads:
            self._access(op, sp, lo, hi, False)
        for (sp, lo, hi) in writes:
            self._access(op, sp, lo, hi, True)
        op.deps.discard(op)
        if dma:
            hist = self.dma_hist[eng]
            if len(hist) >= self.NDMA_SEM:
                op.deps.add(hist[len(hist) - self.NDMA_SEM])
            hist.append(op)
        self.ops.append(op)
        return op

    def _access(self, op, space, lo, hi, is_write):
        recs = self.spaces.get(space)
        if recs is None:
            recs = []
            self.spaces[space] = recs
        deps = op.deps
        if is_write:
            new = []
            for r in recs:
                if r[1] <= lo or r[0] >= hi:
                    new.append(r)
                    continue
                if r[2] is not None:
                    deps.add(r[2])
                deps.update(r[3])
                if r[0] < lo:
                    new.append([r[0], lo, r[2], list(r[3])])
                if r[1] > hi:
                    new.append([hi, r[1], r[2], list(r[3])])
            new.append([lo, hi, op, []])
            self.spaces[space] = new
        else:
            covered = lo
            for r in recs:
                if r[1] <= lo or r[0] >= hi:
                    continue
                if r[2] is not None:
                    deps.add(r[2])
                if op.dma:
                    r[3].append(op)
                else:
                    r[3] = [x for x in r[3] if x.dma or x.eng != op.eng] + [op]
            self._fill_read_gaps(space, lo, hi, op)

    def _fill_read_gaps(self, space, lo, hi, op):
        recs = self.spaces[space]
        ivs = sorted((r[0], r[1]) for r in recs if not (r[1] <= lo or r[0] >= hi))
        cur = lo
        for (a, b) in ivs:
            if a > cur:
                recs.append([cur, a, None, [op]])
            cur = max(cur, b)
        if cur < hi:
            recs.append([cur, hi, None, [op]])

    def emit(self):
        nc = self.nc
        counts = {}
        sems = {}
        n_dma = {"sp": 0, "pool": 0, "act": 0}
        for op in self.ops:
            for d in op.deps:
                if not d.dma and not (d.eng == "pe" and op.eng == "pe" and not op.dma):
                    d.signal = True
        held = []

        def new_sem(name):
            cm = nc.semaphore(name)
            s = cm.__enter__()
            held.append(cm)
            return s

        dma_sems = {}
        for op in self.ops:
            if op.dma:
                i = n_dma[op.eng]
                n_dma[op.eng] += 1
                key = (op.eng, i % self.NDMA_SEM)
                if key not in dma_sems:
                    dma_sems[key] = new_sem("d_%s_%d" % key)
                op.sem = dma_sems[key]
                op.val = 16 * (i // self.NDMA_SEM + 1)
            elif op.signal:
                c = counts.get(op.eng, 0)
                gen = c // self.SEM_ROT
                key = (op.eng, gen)
                if key not in sems:
                    sems[key] = new_sem("c_%s_%d" % key)
                op.sem = sems[key]
                op.val = c % self.SEM_ROT + 1
                counts[op.eng] = c + 1
        seen = {e: {} for e in self.eng}
        for op in self.ops:
            e = self.eng[op.eng]
            sn = seen[op.eng]
            for d in sorted(op.deps, key=lambda o: o.idx):
                if d.eng == "pe" and op.eng == "pe" and not d.dma and not op.dma:
                    continue
                k = d.sem.num
                if sn.get(k, 0) >= d.val:
                    continue
                e.wait_ge(d.sem, d.val)
                sn[k] = d.val
            ins = op.fn() if op.fn is not None else None
            if op.dma:
                ins.then_inc(op.sem, 16)
            elif op.signal:
                ins.then_inc(op.sem, 1)
        final = {}
        for op in self.ops:
            if op.dma:
                final[op.sem.num] = (op.sem, max(op.val, final.get(op.sem.num, (None, 0))[1]))
        for (sem, val) in final.values():
            nc.sync.wait_ge(sem, val)
        return held


class Buf:
    def __init__(self, ap, space, base, esz):
        self.ap = ap
        self.space = space
        self.base = base
        self.esz = esz

    def iv(self, c0, c1):
        return (self.space, self.base + c0 * self.esz, self.base + c1 * self.esz)


def build_program(NSEQ, NT, do_attn=True, do_ffn0=True, do_conv=True, do_ffn1=True):
    T = NMETA + 512 * NT
    NKT = 1 + 4 * NT
    TT = [(0, NMETA)] + [(NMETA + 512 * i, 512) for i in range(NT)]
    KT = [(0, NMETA)] + [(NMETA + 128 * j, 128) for j in range(4 * NT)]
    FFN_GROUPS = [[TT[0], TT[1]]] + [[t] for t in TT[2:]]
    GT = NMETA + 512

    nc = bass.Bass("TRN2", target_bir_lowering=False)
    P = Prog(nc)

    def din(name, shape):
        return nc.dram_tensor(name, list(shape), F32, kind="ExternalInput").ap()

    x_d = din("x", [NSEQ, 512 * NT, D])
    meta_d = din("meta_tokens", [NMETA, D])
    gm_d = din("mixer_norm_g", [2, D])
    gf_d = din("ffn_norm_g", [2, D])
    wqkv_d = din("attn_w_qkv", [D, 3 * D])
    gq_d = din("attn_q_gain", [1, 64])
    gk_d = din("attn_k_gain", [1, 64])
    lq1_d = din("attn_lambda_q1", [1, 64])
    lk1_d = din("attn_lambda_k1", [1, 64])
    lq2_d = din("attn_lambda_q2", [1, 64])
    lk2_d = din("attn_lambda_k2", [1, 64])
    gsub_d = din("attn_sub_gain", [1, 128])
    wo_d = din("attn_w_o", [D, D])
    win_d = din("conv_w_in", [D, 3 * D])
    cw_d = din("conv_w", [3, D])
    wout_d = din("conv_w_out", [D, D])
    wgu_d = din("ffn_w_gate_up", [2, D, 2 * DFF])
    wd_d = din("ffn_w_down", [2, DFF, D])
    ident_d = din("c_ident", [128, 128])
    ones_d = din("c_ones", [128, 128])
    bones_d = din("c_blockones", [128, 128])
    rperm_d = din("c_rperm", [128, 128])
    tri_d = din("c_tri", [128, 128])
    cos_d = din("c_cos", [128, T])
    sin_d = din("c_sin", [128, T])
    out_d = nc.dram_tensor("out", [NSEQ, 512 * NT, D], F32, kind="ExternalOutput").ap()

    def dscr(name, shape):
        return nc.dram_tensor(name, list(shape), BF16, kind="Internal").ap()

    wqkv_b = dscr("wqkv_b", [D, 3 * D])
    wo_b = dscr("wo_b", [D, D])
    win_b = dscr("win_b", [D, 3 * D])
    wout_b = dscr("wout_b", [D, D])
    wgu_b = dscr("wgu_b", [2, D, 2 * DFF])
    wd_b = dscr("wd_b", [2, DFF, D])

    h_t = nc.alloc_sbuf_tensor("h", [128, KC, T], F32)
    xn_t = nc.alloc_sbuf_tensor("xn", [128, KC, T], BF16)
    UBYTES = max(KC * T * 2 + 4 * (T + 2) + 8, 24576 + 16384)
    UBYTES = (UBYTES + 63) // 64 * 64
    TMPBYTES = 20480
    U_t = nc.alloc_sbuf_tensor("U", [128, UBYTES // 2], BF16)
    TMP_t = nc.alloc_sbuf_tensor("TMP", [128, TMPBYTES // 2], BF16)
    W3_t = [nc.alloc_sbuf_tensor("W3_%d" % i, [128, KC, 3, 128], BF16) for i in range(2)]
    W2_t = [nc.alloc_sbuf_tensor("W2_%d" % i, [128, KC, 2, 128], BF16) for i in range(2)]
    W1_t = [nc.alloc_sbuf_tensor("W1_%d" % i, [128, FC, 128], BF16) for i in range(2)]
    cos_t = nc.alloc_sbuf_tensor("cos", [128, T], F32)
    sin_t = nc.alloc_sbuf_tensor("sin", [128, T], F32)
    ident_t = nc.alloc_sbuf_tensor("ident", [128, 128], F32)
    identb_t = nc.alloc_sbuf_tensor("identb", [128, 128], BF16)
    ones_t = nc.alloc_sbuf_tensor("ones", [128, 128], BF16)
    bones_t = nc.alloc_sbuf_tensor("bones", [128, 128], BF16)
    rperm_t = nc.alloc_sbuf_tensor("rperm", [128, 128], BF16)
    tri_t = nc.alloc_sbuf_tensor("tri", [128, 128], BF16)
    metaT_t = nc.alloc_sbuf_tensor("metaT", [128, KC, NMETA], F32)
    gm_t = nc.alloc_sbuf_tensor("gm", [128, 2, KC], F32)
    gf_t = nc.alloc_sbuf_tensor("gf", [128, 2, KC], F32)
    cw_t = nc.alloc_sbuf_tensor("cw", [128, 3, KC], F32)
    sm_t = nc.alloc_sbuf_tensor("sm", [128, 16], F32)
    ps_t = nc.alloc_psum_tensor("ps", [128, 8, 512], F32)

    def sv(tname, lo, hi):
        return (tname, lo, hi)

    def whole(tname):
        return (tname, 0, 1 << 30)

    def ivh(c, t0, n):
        return ("h", (c * T + t0) * 4, (c * T + t0 + n) * 4)

    def ivxn(c, t0, n):
        return ("xn", (c * T + t0) * 2, (c * T + t0 + n) * 2)

    def ivps(b):
        return ("ps", b, b + 1)

    def uview(off, ncols, dt):
        esz = 4 if dt == F32 else 2
        assert off % 4 == 0 and off + ncols * esz <= UBYTES, (off, ncols, esz, UBYTES)
        a = U_t[:, off // 2: off // 2 + ncols * esz // 2]
        if dt == F32:
            a = a.bitcast(F32)
        return Buf(a, "U", off, esz)

    def tview(off, ncols, dt):
        esz = 4 if dt == F32 else 2
        assert off % 4 == 0 and off + ncols * esz <= TMPBYTES, (off, ncols, esz)
        a = TMP_t[:, off // 2: off // 2 + ncols * esz // 2]
        if dt == F32:
            a = a.bitcast(F32)
        return Buf(a, "TMP", off, esz)

    qT = uview(0, T, BF16)
    kT = uview(2 * T, T, BF16)
    vT = uview(4 * T, T, BF16)
    Vtm = uview(6 * T, NKT * 128, BF16)
    kT1 = uview(6 * T + NKT * 256, T, BF16)
    act = uview(0, FC * GT, BF16)
    yb = uview(0, KC * T, BF16)
    zb = uview(KC * T * 2, T + 2, F32)
    xs = [uview(24576 + 4096 * i, D, F32) for i in range(2)]
    osb = [uview(24576 + 8192 + 4096 * i, D, F32) for i in range(2)]
    n_sqb = tview(0, KC * 512, BF16)
    n_lnv = tview(8192, 512, F32)
    n_rstd = [tview(10240 + 2048 * i, 512, F32) for i in range(2)]
    r_sq = [tview(0 + 1024 * i, 512, BF16) for i in range(2)]
    r_rawb = [tview(2048 + 1024 * i, 512, BF16) for i in range(2)]
    r_t1 = [tview(4096 + 2048 * i, 512, F32) for i in range(2)]
    r_t2 = [tview(8192 + 2048 * i, 512, F32) for i in range(2)]
    r_lnv = tview(12288, 512, F32)
    r_rstd = [tview(14336 + 2048 * i, 512, F32) for i in range(2)]
    a_pT = [tview(1024 * i, 512, BF16) for i in range(4)]
    a_r = [tview(4096 + 2048 * i, 512, F32) for i in range(2)]
    a_a = [tview(8192 + 2048 * i, 512, F32) for i in range(2)]
    a_o = tview(12288, 512, F32)
    a_lnv = tview(14336, 512, F32)
    a_rstd = tview(16384, 512, F32)
    a_sq = tview(18432, 512, BF16)
    a_ao = tview(19456, 512, BF16)
    f_sg = [tview(2048 * i, 512, F32) for i in range(2)]
    c_cs = [tview(2048 * i, 512, F32) for i in range(2)]
    c_tmp = [tview(4096 + 2048 * i, 512, F32) for i in range(2)]

    lamb = tview(0, 256, F32)
    lam_t = lamb.ap.rearrange("p (i d) -> p i d", i=4)
    h3 = h_t[:, :, :]
    xn3 = xn_t[:, :, :]
    ps = ps_t[:, :, :]

    bank_ctr = {"A": 0, "B": 0, "ALL": 0, "S": 0, "X": 0}

    def bank(pool):
        i = bank_ctr[pool]
        bank_ctr[pool] = i + 1
        if pool == "A":
            return i % 4
        if pool == "B":
            return 4 + i % 4
        if pool == "S":
            return 4 + i % 3
        if pool == "X":
            return 7
        return i % 8

    def dma(eng, out_ap, in_ap, reads, writes, slow=False):
        e = P.eng[eng]
        if slow:
            fn = lambda: e.dma_start(out=out_ap, in_=in_ap, allow_slow_non_contiguous=True)
        else:
            fn = lambda: e.dma_start(out=out_ap, in_=in_ap)
        return P.add(eng, fn, reads=reads, writes=writes, dma=True)

    def mm_group(out_ap, out_iv, pairs, reads, start=True, stop=True):
        n = len(pairs)

        def fn():
            ins = None
            for i, (l, r) in enumerate(pairs):
                ins = nc.tensor.matmul(out_ap, l, r, start=(start and i == 0),
                                       stop=(stop and i == n - 1))
            return ins
        rd = list(reads)
        if not start:
            rd.append(out_iv)
        return P.add("pe", fn, reads=rd, writes=[out_iv])

    def act_op(out_ap, in_ap, func, reads, writes, bias=None, scale=1.0):
        if bias is None:
            fn = lambda: nc.scalar.activation(out=out_ap, in_=in_ap, func=func, scale=scale)
        else:
            fn = lambda: nc.scalar.activation(out=out_ap, in_=in_ap, func=func, bias=bias,
                                              scale=scale)
        return P.add("act", fn, reads=reads, writes=writes)

    def stt(eng, out_ap, in0, scalar, in1, op0, op1, reads, writes):
        e = P.eng[eng]
        return P.add(eng, lambda: e.scalar_tensor_tensor(out=out_ap, in0=in0, scalar=scalar,
                                                         in1=in1, op0=op0, op1=op1),
                     reads=reads, writes=writes)

    def tt(eng, out_ap, in0, in1, op, reads, writes):
        e = P.eng[eng]
        return P.add(eng, lambda: e.tensor_tensor(out=out_ap, in0=in0, in1=in1, op=op),
                     reads=reads, writes=writes)

    def tcopy(eng, out_ap, in_ap, reads, writes):
        if eng == "act":
            return P.add("act", lambda: nc.scalar.copy(out=out_ap, in_=in_ap), reads=reads,
                         writes=writes)
        e = P.eng[eng]
        return P.add(eng, lambda: e.tensor_copy(out_ap, in_ap), reads=reads, writes=writes)

    SM = "sm"

    def smc(i):
        return sm_t[:, i:i + 1]

    def cast_dma(out_ap, in_ap, reads, writes):
        return dma("pool", out_ap, in_ap, reads, writes)

    def cast_rows(dst, src, name, nrows, step):
        for r0 in range(0, nrows, step):
            r1 = min(nrows, r0 + step)
            cast_dma(dst[r0:r1, :], src[r0:r1, :], [], [(name, r0, r1)])

    cast_rows(wqkv_b, wqkv_d, "wqkv_b", D, 256)
    cast_rows(wo_b, wo_d, "wo_b", D, 512)
    for l in range(2):
        cast_rows(wgu_b[l], wgu_d[l], "wgu_b%d" % l, D, 128)
        cast_rows(wd_b[l], wd_d[l], "wd_b%d" % l, DFF, 704)
        if l == 0:
            cast_rows(win_b, win_d, "win_b", D, 256)
            cast_rows(wout_b, wout_d, "wout_b", D, 512)

    dma("sp", ident_t[:, :], ident_d[:, :], [], [whole("ident")])
    cast_dma(identb_t[:, :], ident_d[:, :], [], [whole("identb")])
    cast_dma(ones_t[:, :], ones_d[:, :], [], [whole("ones")])
    cast_dma(bones_t[:, :], bones_d[:, :], [], [whole("bones")])
    cast_dma(rperm_t[:, :], rperm_d[:, :], [], [whole("rperm")])
    cast_dma(tri_t[:, :], tri_d[:, :], [], [whole("tri")])
    dma("sp", xs[0].ap[0:NMETA, :], meta_d[:, :], [], [xs[0].iv(0, D)])
    for half in range(2):
        b = bank("ALL")
        c0 = half * 4

        def fn(b=b, c0=c0):
            ins = None
            for cc in range(4):
                ins = nc.tensor.transpose(ps[:, b, cc * 128: cc * 128 + NMETA],
                                          xs[0].ap[0:NMETA, (c0 + cc) * 128:(c0 + cc + 1) * 128],
                                          ident_t[0:NMETA, 0:NMETA])
            return ins
        P.add("pe", fn, reads=[xs[0].iv(0, D), whole("ident")], writes=[ivps(b)])
        tcopy("dve", metaT_t[:, c0:c0 + 4, :],
              ps[:, b, :].rearrange("p (c n) -> p c n", c=4)[:, :, 0:NMETA],
              [ivps(b)], [("metaT", c0, c0 + 4)])

    def prologue_b():
        dma("sp", cos_t[:, :], cos_d[:, :], [], [whole("cos")])
        dma("sp", sin_t[:, :], sin_d[:, :], [], [whole("sin")])
        for l in range(2):
            dma("sp", gm_t[:, l, :], gm_d[l, :].rearrange("(c p) -> p c", p=128), [],
                [("gm", l, l + 1)], slow=True)
            dma("sp", gf_t[:, l, :], gf_d[l, :].rearrange("(c p) -> p c", p=128), [],
                [("gf", l, l + 1)], slow=True)
        for w in range(3):
            dma("sp", cw_t[:, w, :], cw_d[w, :].rearrange("(c p) -> p c", p=128), [],
                [("cw", w, w + 1)], slow=True)

        def col64(src, lo):
            return src[0, lo:lo + 32].rearrange("(d o) -> d o", o=1)

        for (col, src) in ((0, gq_d), (2, gk_d)):
            for half in range(2):
                for q in range(2):
                    p0 = half * 64 + q * 32
                    dma("sp", sm_t[p0:p0 + 32, col:col + 1], col64(src, q * 32), [],
                        [(SM, col * 1000 + p0, col * 1000 + p0 + 32)], slow=True)
                    dma("sp", sm_t[p0:p0 + 32, col + 1:col + 2], col64(src, (1 - q) * 32), [],
                        [(SM, (col + 1) * 1000 + p0, (col + 1) * 1000 + p0 + 32)], slow=True)
        dma("sp", sm_t[:, 4:5], gsub_d[0, :].rearrange("(d o) -> d o", o=1), [],
            [(SM, 4000, 4128)], slow=True)
        for i, src in enumerate((lq1_d, lk1_d, lq2_d, lk2_d)):
            dma("sp", lam_t[:, i, :], src[0:1, :].partition_broadcast(128), [],
                [lamb.iv(i * 64, (i + 1) * 64)])
        P.add("dve", lambda: nc.vector.memset(sm_t[:, 5:6], EPS), writes=[(SM, 5000, 5128)])
        P.add("dve", lambda: nc.vector.tensor_scalar(out=sm_t[:, 0:2], in0=sm_t[:, 0:2],
                                                     scalar1=0.125, scalar2=None, op0=ALU.mult),
              reads=[(SM, 0, 2000)], writes=[(SM, 0, 2000)])
        P.add("dve", lambda: nc.vector.tensor_scalar(out=sm_t[:, 4:5], in0=sm_t[:, 4:5],
                                                     scalar1=1.0 - LAMBDA_INIT0, scalar2=None,
                                                     op0=ALU.mult),
              reads=[(SM, 4000, 4128)], writes=[(SM, 4000, 4128)])
        for i in range(2):
            P.add("dve", lambda i=i: nc.vector.tensor_tensor(out=lam_t[:, 2 * i, :],
                                                             in0=lam_t[:, 2 * i, :],
                                                             in1=lam_t[:, 2 * i + 1, :],
                                                             op=ALU.mult),
                  reads=[lamb.iv(2 * i * 64, (2 * i + 2) * 64)], writes=[lamb.iv(2 * i * 64, (2 * i + 1) * 64)])
            P.add("dve", lambda i=i: nc.vector.tensor_reduce(out=sm_t[:, 9 + i:10 + i],
                                                             in_=lam_t[:, 2 * i, :], axis=AX.X,
                                                             op=ALU.add),
                  reads=[lamb.iv(2 * i * 64, (2 * i + 1) * 64)], writes=[(SM, (9 + i) * 1000, (9 + i) * 1000 + 128)])
            act_op(sm_t[:, 7 + i:8 + i], sm_t[:, 9 + i:10 + i], AF.Exp,
                   [(SM, (9 + i) * 1000, (9 + i) * 1000 + 128)],
                   [(SM, (7 + i) * 1000, (7 + i) * 1000 + 128)])
        P.add("dve", lambda: nc.vector.tensor_tensor(out=sm_t[:, 6:7], in0=sm_t[:, 8:9],
                                                     in1=sm_t[:, 7:8], op=ALU.subtract),
              reads=[(SM, 7000, 8128)], writes=[(SM, 6000, 6128)])
        P.add("dve", lambda: nc.vector.tensor_scalar(out=sm_t[:, 6:7], in0=sm_t[:, 6:7],
                                                     scalar1=-LAMBDA_INIT0, scalar2=None,
                                                     op0=ALU.add),
              reads=[(SM, 6000, 6128)], writes=[(SM, 6000, 6128)])


    wslot = {"W3": 0, "W2": 0, "W1": 0, "WO": 0}

    def next_slot(kind):
        i = wslot[kind]
        wslot[kind] = i + 1
        return i % 2

    def rmsnorm(g_t, l, tiles):
        for ti, (t0, n) in enumerate(tiles):
            sq3 = n_sqb.ap.rearrange("p (c n) -> p c n", c=KC)[:, :, 0:n]
            act_op(sq3, h3[:, :, t0:t0 + n], AF.Square,
                   [ivh(c, t0, n) for c in range(KC)], [n_sqb.iv(0, KC * 512)])
            b = bank("B")
            mm_group(ps[:, b, 0:n], ivps(b),
                     [(ones_t[:, :], sq3[:, c, :]) for c in range(KC)],
                     [whole("ones"), n_sqb.iv(0, KC * 512)])
            act_op(n_lnv.ap[:, 0:n], ps[:, b, 0:n], AF.Ln, [ivps(b), (SM, 5000, 5128)],
                   [n_lnv.iv(0, n)], bias=smc(5), scale=1.0 / D)
            rs = n_rstd[ti % 2]
            act_op(rs.ap[:, 0:n], n_lnv.ap[:, 0:n], AF.Exp, [n_lnv.iv(0, n)], [rs.iv(0, n)],
                   scale=-0.5)
            for c in range(KC):
                stt("dve", xn3[:, c, t0:t0 + n], h3[:, c, t0:t0 + n], g_t[:, l, c:c + 1],
                    rs.ap[:, 0:n], ALU.mult, ALU.mult,
                    [ivh(c, t0, n), rs.iv(0, n), whole("gm"), whole("gf")], [ivxn(c, t0, n)])

    def load_w3(src_b, srcname, blk):
        s = next_slot("W3")
        v = src_b.rearrange("(kc p) n -> p kc n", p=128)
        for which in range(3):
            c0 = which * D + blk * 128
            dma("sp", W3_t[s][:, :, which, :], v[:, :, c0:c0 + 128], [whole(srcname)],
                [("W3_%d" % s, which, which + 1)])
        return s

    def ffn(l, tiles_groups):
        gname = "wgu_b%d" % l
        dname = "wd_b%d" % l
        gu_v = wgu_b[l].rearrange("(kc p) n -> p kc n", p=128)
        wd_v = wd_b[l].rearrange("(j p) n -> p j n", p=128)
        for grp in tiles_groups:
            rmsnorm(gf_t, l, grp)
            offs = []
            o = 0
            for (t0, n) in grp:
                offs.append(o)
                o += n
            act3 = act.ap.rearrange("p (j t) -> p j t", j=FC)
            pending = None

            def load_gu(j):
                s = next_slot("W2")
                dma("sp", W2_t[s][:, :, 0, :], gu_v[:, :, j * 128:(j + 1) * 128],
                    [whole(gname)], [("W2_%d" % s, 0, 1)])
                dma("sp", W2_t[s][:, :, 1, :], gu_v[:, :, DFF + j * 128: DFF + (j + 1) * 128],
                    [whole(gname)], [("W2_%d" % s, 1, 2)])
                return s

            def load_d(c):
                s = next_slot("W1")
                dma("sp", W1_t[s][:, :, :], wd_v[:, :, c * 128:(c + 1) * 128], [whole(dname)],
                    [whole("W1_%d" % s)])
                return s

            slots = [load_gu(0)]
            for j in range(FC):
                if j + 1 < FC:
                    slots.append(load_gu(j + 1))
                s = slots[j]
                for gi, (t0, n) in enumerate(grp):
                    bg = bank("B")
                    bu = bank("B")
                    rhs_reads = [ivxn(c, t0, n) for c in range(KC)]
                    mm_group(ps[:, bg, 0:n], ivps(bg),
                             [(W2_t[s][:, kc, 0, :], xn3[:, kc, t0:t0 + n]) for kc in range(KC)],
                             rhs_reads + [("W2_%d" % s, 0, 1)])
                    mm_group(ps[:, bu, 0:n], ivps(bu),
                             [(W2_t[s][:, kc, 1, :], xn3[:, kc, t0:t0 + n]) for kc in range(KC)],
                             rhs_reads + [("W2_%d" % s, 1, 2)])
                    sg = f_sg[(j * len(grp) + gi) % 2]
                    act_op(sg.ap[:, 0:n], ps[:, bg, 0:n], AF.Silu, [ivps(bg)], [sg.iv(0, n)])
                    o0 = j * GT + offs[gi]
                    tt("dve", act3[:, j, offs[gi]:offs[gi] + n], sg.ap[:, 0:n], ps[:, bu, 0:n],
                       ALU.mult, [sg.iv(0, n), ivps(bu)], [act.iv(o0, o0 + n)])
            dslots = [load_d(0)]
            for c in range(KC):
                if c + 1 < KC:
                    dslots.append(load_d(c + 1))
                s = dslots[c]
                for gi, (t0, n) in enumerate(grp):
                    b = bank("A")
                    mm_group(ps[:, b, 0:n], ivps(b),
                             [(W1_t[s][:, j, :], act3[:, j, offs[gi]:offs[gi] + n])
                              for j in range(FC)],
                             [whole("W1_%d" % s)] +
                             [act.iv(j * GT + offs[gi], j * GT + offs[gi] + n) for j in range(FC)])
                    tt("dve", h3[:, c, t0:t0 + n], h3[:, c, t0:t0 + n], ps[:, b, 0:n], ALU.add,
                       [ivh(c, t0, n), ivps(b)], [ivh(c, t0, n)])

    def attention():
        P.add("dve", lambda: nc.vector.memset(kT.ap[64:128, :], 0.0), writes=[kT.iv(0, T)])
        P.add("dve", lambda: nc.vector.memset(kT1.ap[0:64, :], 0.0), writes=[kT1.iv(0, T)])
        rmsnorm(gm_t, 0, TT)
        wo_v = wo_b
        nxt = load_w3(wqkv_b, "wqkv_b", 0)
        for hd in range(H):
            s3 = nxt
            so = next_slot("W1")
            dma("sp", W1_t[so][:, 0:KC, :],
                wo_v[hd * 128:(hd + 1) * 128, :].rearrange("p (c n) -> p c n", c=KC),
                [whole("wo_b")], [whole("W1_%d" % so)])
            if hd + 1 < H:
                nxt = load_w3(wqkv_b, "wqkv_b", hd + 1)
            it = 0
            pendB = []
            for (t0, n) in TT:
                rhs_reads = [ivxn(c, t0, n) for c in range(KC)]
                for which in (2, 0, 1):
                    bp = bank("ALL")
                    mm_group(ps[:, bp, 0:n], ivps(bp),
                             [(W3_t[s3][:, kc, which, :], xn3[:, kc, t0:t0 + n])
                              for kc in range(KC)],
                             rhs_reads + [("W3_%d" % s3, which, which + 1)])
                    if which == 2:
                        tcopy("act", vT.ap[:, t0:t0 + n], ps[:, bp, 0:n], [ivps(bp)],
                              [vT.iv(t0, t0 + n)])
                        if pendB:
                            pendB.pop(0)()
                        continue
                    k2 = it % 2
                    it += 1
                    gcol = 0 if which == 0 else 2
                    sq, rawb, t1, t2, rs = r_sq[k2], r_rawb[k2], r_t1[k2], r_t2[k2], r_rstd[k2]
                    act_op(sq.ap[:, 0:n], ps[:, bp, 0:n], AF.Square, [ivps(bp)], [sq.iv(0, n)])
                    act_op(t1.ap[:, 0:n], ps[:, bp, 0:n], AF.Copy, [ivps(bp), (SM, 0, 4000)],
                           [t1.iv(0, n)], scale=smc(gcol))
                    act_op(rawb.ap[:, 0:n], ps[:, bp, 0:n], AF.Copy, [ivps(bp), (SM, 0, 4000)],
                           [rawb.iv(0, n)], scale=smc(gcol))
                    tt("dve", t1.ap[:, 0:n], t1.ap[:, 0:n], cos_t[:, t0:t0 + n], ALU.mult,
                       [t1.iv(0, n), whole("cos")], [t1.iv(0, n)])
                    if pendB:
                        pendB.pop(0)()

                    def stageB(which=which, t0=t0, n=n, sq=sq, rawb=rawb, t1=t1, t2=t2, rs=rs):
                        bs = bank("ALL")
                        mm_group(ps[:, bs, 0:n], ivps(bs), [(bones_t[:, :], sq.ap[:, 0:n])],
                                 [whole("bones"), sq.iv(0, n)])
                        br = bank("ALL")
                        mm_group(ps[:, br, 0:n], ivps(br), [(rperm_t[:, :], rawb.ap[:, 0:n])],
                                 [whole("rperm"), rawb.iv(0, n)])
                        act_op(r_lnv.ap[:, 0:n], ps[:, bs, 0:n], AF.Ln,
                               [ivps(bs), (SM, 5000, 5128)], [r_lnv.iv(0, n)], bias=smc(5),
                               scale=1.0 / 64)
                        act_op(rs.ap[:, 0:n], r_lnv.ap[:, 0:n], AF.Exp, [r_lnv.iv(0, n)],
                               [rs.iv(0, n)], scale=-0.5)
                        tt("dve", t2.ap[:, 0:n], sin_t[:, t0:t0 + n], ps[:, br, 0:n], ALU.mult,
                           [ivps(br), whole("sin")], [t2.iv(0, n)])
                        tt("dve", t1.ap[:, 0:n], t1.ap[:, 0:n], t2.ap[:, 0:n], ALU.add,
                           [t1.iv(0, n), t2.iv(0, n)], [t1.iv(0, n)])
                        if which == 0:
                            tt("dve", qT.ap[:, t0:t0 + n], t1.ap[:, 0:n], rs.ap[:, 0:n],
                               ALU.mult, [t1.iv(0, n), rs.iv(0, n)], [qT.iv(t0, t0 + n)])
                        else:
                            tt("dve", kT.ap[0:64, t0:t0 + n], t1.ap[0:64, 0:n],
                               rs.ap[0:64, 0:n], ALU.mult, [t1.iv(0, n), rs.iv(0, n)],
                               [kT.iv(t0, t0 + n)])
                            tt("dve", kT1.ap[64:128, t0:t0 + n], t1.ap[64:128, 0:n],
                               rs.ap[64:128, 0:n], ALU.mult, [t1.iv(0, n), rs.iv(0, n)],
                               [kT1.iv(t0, t0 + n)])
                    pendB.append(stageB)
            while pendB:
                pendB.pop(0)()
            if DEBUG_ATTN_STAGE < 2:
                continue
            vt3 = Vtm.ap.rearrange("p (j d) -> p j d", d=128)
            b = bank("ALL")
            psb = ps[:, b, 0:256].bitcast(BF16)
            P.add("pe", lambda psb=psb: nc.tensor.transpose(psb[0:NMETA, 0:128],
                                                            vT.ap[:, 0:NMETA], identb_t[:, :]),
                  reads=[vT.iv(0, NMETA), whole("identb")], writes=[ivps(b)])
            tcopy("dve", vt3[0:NMETA, 0, :], psb[0:NMETA, 0:128], [ivps(b)], [Vtm.iv(0, 128)])
            for j0 in range(1, NKT, 4):
                b = bank("ALL")
                psb = ps[:, b, 0:256].bitcast(BF16)

                def fn(psb=psb, j0=j0):
                    ins = None
                    for jj in range(4):
                        k0 = KT[j0 + jj][0]
                        ins = nc.tensor.transpose(psb[:, jj * 128:(jj + 1) * 128],
                                                  vT.ap[:, k0:k0 + 128], identb_t[:, :])
                    return ins
                kk0 = KT[j0][0]
                P.add("pe", fn, reads=[vT.iv(kk0, kk0 + 512), whole("identb")], writes=[ivps(b)])
                tcopy("dve" if (j0 // 4) % 2 == 0 else "act", vt3[:, j0:j0 + 4, :],
                      psb.rearrange("p (j d) -> p j d", d=128), [ivps(b)],
                      [Vtm.iv(j0 * 128, (j0 + 4) * 128)])
            pend = []

            def flush_some(k):
                for _ in range(k):
                    if pend:
                        pend.pop(0)()
            for (t0, n) in TT:
                if DEBUG_ATTN_STAGE < 3:
                    continue
                bo = [bank("A"), bank("A")]
                bsum = [bank("A"), bank("A")]
                items = []
                for j, (k0, kn) in enumerate(KT):
                    if k0 >= t0 + n:
                        break
                    diag = k0 >= t0
                    qs = (k0 - t0) if diag else 0
                    for m in range(2):
                        items.append((j, k0, kn, diag, qs, m))

                def emit_st(item):
                    j, k0, kn, diag, qs, m = item
                    nq = n - qs
                    bst = bank("S")
                    mm_group(ps[0:kn, bst, 0:nq], ivps(bst),
                             [((kT if m == 0 else kT1).ap[:, k0:k0 + kn],
                               qT.ap[:, t0 + qs:t0 + n])],
                             [(kT if m == 0 else kT1).iv(k0, k0 + kn), qT.iv(t0 + qs, t0 + n)])
                    return bst

                def emit_av(item, bst, idx):
                    j, k0, kn, diag, qs, m = item
                    nq = n - qs
                    pT = a_pT[idx % 4]
                    act_op(pT.ap[0:kn, 0:nq], ps[0:kn, bst, 0:nq], AF.Exp, [ivps(bst)],
                           [pT.iv(0, nq)])
                    if diag:
                        tt("dve", pT.ap[0:kn, 0:kn], pT.ap[0:kn, 0:kn], tri_t[0:kn, 0:kn],
                           ALU.mult, [pT.iv(0, kn), whole("tri")], [pT.iv(0, kn)])
                    first = (j == 0)
                    last = (j == items[-1][0])
                    mm_group(ps[:, bo[m], qs:n], ivps(bo[m]),
                             [(vt3[0:kn, j, :], pT.ap[0:kn, 0:nq])],
                             [Vtm.iv(j * 128, (j + 1) * 128), pT.iv(0, nq)], start=first,
                             stop=last)
                    mm_group(ps[:, bsum[m], qs:n], ivps(bsum[m]),
                             [(ones_t[0:kn, :], pT.ap[0:kn, 0:nq])],
                             [whole("ones"), pT.iv(0, nq)], start=first, stop=last)

                LOOK = 2
                sts = []
                for i in range(min(LOOK, len(items))):
                    sts.append(emit_st(items[i]))
                for i, item in enumerate(items):
                    if i + LOOK < len(items):
                        sts.append(emit_st(items[i + LOOK]))
                    emit_av(item, sts[i], i)
                    if i >= 1:
                        flush_some(1)
                flush_some(len(pend))
                if DEBUG_ATTN_STAGE < 4:
                    continue
                for m in range(2):
                    act_op(a_r[m].ap[:, 0:n], ps[:, bsum[m], 0:n], AF.Ln, [ivps(bsum[m])],
                           [a_r[m].iv(0, n)])
                    act_op(a_r[m].ap[:, 0:n], a_r[m].ap[:, 0:n], AF.Exp, [a_r[m].iv(0, n)],
                           [a_r[m].iv(0, n)], scale=-1.0)
                    tt("dve", a_a[m].ap[:, 0:n], a_r[m].ap[:, 0:n], ps[:, bo[m], 0:n], ALU.mult,
                       [ivps(bo[m]), a_r[m].iv(0, n)], [a_a[m].iv(0, n)])
                stt("dve", a_o.ap[:, 0:n], a_a[1].ap[:, 0:n], smc(6), a_a[0].ap[:, 0:n],
                    ALU.mult, ALU.add, [a_a[0].iv(0, n), a_a[1].iv(0, n), (SM, 6000, 6128)],
                    [a_o.iv(0, n)])
                act_op(a_sq.ap[:, 0:n], a_o.ap[:, 0:n], AF.Square, [a_o.iv(0, n)],
                       [a_sq.iv(0, n)])

                def f1(t0=t0, n=n):
                    bq = bank("X")
                    mm_group(ps[:, bq, 0:n], ivps(bq), [(ones_t[:, :], a_sq.ap[:, 0:n])],
                             [whole("ones"), a_sq.iv(0, n)])
                    act_op(a_lnv.ap[:, 0:n], ps[:, bq, 0:n], AF.Ln, [ivps(bq), (SM, 5000, 5128)],
                           [a_lnv.iv(0, n)], bias=smc(5), scale=1.0 / 128)
                    act_op(a_rstd.ap[:, 0:n], a_lnv.ap[:, 0:n], AF.Exp, [a_lnv.iv(0, n)],
                           [a_rstd.iv(0, n)], scale=-0.5)
                    stt("dve", a_ao.ap[:, 0:n], a_o.ap[:, 0:n], smc(4), a_rstd.ap[:, 0:n],
                        ALU.mult, ALU.mult, [a_o.iv(0, n), a_rstd.iv(0, n), (SM, 4000, 4128)],
                        [a_ao.iv(0, n)])
                pend.append(f1)
                for c in range(KC):
                    def f2(c=c, t0=t0, n=n, so=so):
                        b = bank("X")
                        mm_group(ps[:, b, 0:n], ivps(b), [(W1_t[so][:, c, :], a_ao.ap[:, 0:n])],
                                 [whole("W1_%d" % so), a_ao.iv(0, n)])
                        tt("dve", h3[:, c, t0:t0 + n], h3[:, c, t0:t0 + n], ps[:, b, 0:n],
                           ALU.add, [ivh(c, t0, n), ivps(b)], [ivh(c, t0, n)])
                    pend.append(f2)
            flush_some(len(pend))

    def conv():
        rmsnorm(gm_t, 1, TT)
        yb3 = yb.ap.rearrange("p (c t) -> p c t", c=KC)
        nxt = load_w3(win_b, "win_b", 0)
        P.add("pool", lambda: nc.gpsimd.memset(zb.ap[:, 0:2], 0.0), writes=[zb.iv(0, 2)])
        it = 0
        for c in range(KC):
            s3 = nxt
            if c + 1 < KC:
                nxt = load_w3(win_b, "win_b", c + 1)
            for (t0, n) in TT:
                rhs_reads = [ivxn(cc, t0, n) for cc in range(KC)]
                bks = []
                for which in range(3):
                    bp = bank("ALL")
                    bks.append(bp)
                    mm_group(ps[:, bp, 0:n], ivps(bp),
                             [(W3_t[s3][:, kc, which, :], xn3[:, kc, t0:t0 + n])
                              for kc in range(KC)],
                             rhs_reads + [("W3_%d" % s3, which, which + 1)])
                k2 = it % 2
                it += 1
                cs, tmp = c_cs[k2], c_tmp[k2]
                tcopy("act", cs.ap[:, 0:n], ps[:, bks[1], 0:n], [ivps(bks[1])], [cs.iv(0, n)])
                tt("dve", zb.ap[:, 2 + t0:2 + t0 + n], cs.ap[:, 0:n], ps[:, bks[2], 0:n],
                   ALU.mult, [cs.iv(0, n), ivps(bks[2])], [zb.iv(2 + t0, 2 + t0 + n)])
                P.add("dve", lambda t0=t0, n=n, tmp=tmp, c=c: nc.vector.tensor_scalar(
                    out=tmp.ap[:, 0:n], in0=zb.ap[:, t0:t0 + n], scalar1=cw_t[:, 0, c:c + 1],
                    scalar2=None, op0=ALU.mult),
                    reads=[zb.iv(t0, t0 + n), whole("cw")], writes=[tmp.iv(0, n)])
                stt("dve", tmp.ap[:, 0:n], zb.ap[:, 1 + t0:1 + t0 + n], cw_t[:, 1, c:c + 1],
                    tmp.ap[:, 0:n], ALU.mult, ALU.add,
                    [zb.iv(1 + t0, 1 + t0 + n), tmp.iv(0, n), whole("cw")], [tmp.iv(0, n)])
                stt("dve", tmp.ap[:, 0:n], zb.ap[:, 2 + t0:2 + t0 + n], cw_t[:, 2, c:c + 1],
                    tmp.ap[:, 0:n], ALU.mult, ALU.add,
                    [zb.iv(2 + t0, 2 + t0 + n), tmp.iv(0, n), whole("cw")], [tmp.iv(0, n)])
                tt("dve", yb3[:, c, t0:t0 + n], tmp.ap[:, 0:n], ps[:, bks[0], 0:n], ALU.mult,
                   [tmp.iv(0, n), ivps(bks[0])], [yb.iv(c * T + t0, c * T + t0 + n)])
        wout_v = wout_b.rearrange("(kc p) n -> p kc n", p=128)

        def load_wout(co):
            s = next_slot("W1")
            dma("sp", W1_t[s][:, 0:KC, :], wout_v[:, :, co * 128:(co + 1) * 128],
                [whole("wout_b")], [whole("W1_%d" % s)])
            return s
        slots = [load_wout(0)]
        for co in range(KC):
            if co + 1 < KC:
                slots.append(load_wout(co + 1))
            s = slots[co]
            for (t0, n) in TT:
                b = bank("ALL")
                mm_group(ps[:, b, 0:n], ivps(b),
                         [(W1_t[s][:, c, :], yb3[:, c, t0:t0 + n]) for c in range(KC)],
                         [whole("W1_%d" % s)] +
                         [yb.iv(c * T + t0, c * T + t0 + n) for c in range(KC)])
                tt("dve", h3[:, co, t0:t0 + n], h3[:, co, t0:t0 + n], ps[:, b, 0:n], ALU.add,
                   [ivh(co, t0, n), ivps(b)], [ivh(co, t0, n)])

    def load_x(s):
        tcopy("pool", h3[:, :, 0:NMETA], metaT_t[:, :, :], [whole("metaT")],
              [ivh(c, 0, NMETA) for c in range(KC)])
        for blk in range(4 * NT):
            t0 = NMETA + blk * 128
            xb = xs[blk % 2]
            dma("sp", xb.ap[:, :], x_d[s, blk * 128:(blk + 1) * 128, :], [], [xb.iv(0, D)])
            for half in range(2):
                b = bank("ALL")
                c0 = half * 4

                def fn(b=b, c0=c0, xb=xb):
                    ins = None
                    for cc in range(4):
                        ins = nc.tensor.transpose(ps[:, b, cc * 128:(cc + 1) * 128],
                                                  xb.ap[:, (c0 + cc) * 128:(c0 + cc + 1) * 128],
                                                  ident_t[:, :])
                    return ins
                P.add("pe", fn, reads=[xb.iv(0, D), whole("ident")], writes=[ivps(b)])
                tcopy("act" if half == 0 else "dve", h3[:, c0:c0 + 4, t0:t0 + 128],
                      ps[:, b, :].rearrange("p (c n) -> p c n", c=4), [ivps(b)],
                      [ivh(c, t0, 128) for c in range(c0, c0 + 4)])

    def store_out(s):
        for blk in range(4 * NT):
            t0 = NMETA + blk * 128
            ob = osb[blk % 2]
            for half in range(2):
                b = bank("ALL")
                c0 = half * 4

                def fn(b=b, c0=c0, t0=t0):
                    ins = None
                    for cc in range(4):
                        ins = nc.tensor.transpose(ps[:, b, cc * 128:(cc + 1) * 128],
                                                  h3[:, c0 + cc, t0:t0 + 128], ident_t[:, :])
                    return ins
                P.add("pe", fn, reads=[ivh(c, t0, 128) for c in range(c0, c0 + 4)] +
                      [whole("ident")], writes=[ivps(b)])
                tcopy("act" if half == 0 else "dve", ob.ap[:, c0 * 128:(c0 + 4) * 128],
                      ps[:, b, :], [ivps(b)], [ob.iv(c0 * 128, (c0 + 4) * 128)])
            dma("sp", out_d[s, blk * 128:(blk + 1) * 128, :], ob.ap[:, :], [ob.iv(0, D)],
                [("out", (s * 4 * NT + blk), (s * 4 * NT + blk) + 1)])

    for s in range(NSEQ):
        load_x(s)
        if s == 0:
            prologue_b()
        if do_attn:
            attention()
        if do_ffn0:
            ffn(0, FFN_GROUPS)
        if do_conv:
            conv()
        if do_ffn1:
            ffn(1, FFN_GROUPS)
        store_out(s)
    P.add("sp", None, reads=[("out", 0, 1 << 30)])
    held = P.emit()
    return nc, held, len(P.ops)


def make_consts(T):
    ident = np.eye(128, dtype=np.float32)
    ones = np.ones((128, 128), np.float32)
    bones = np.zeros((128, 128), np.float32)
    bones[:64, :64] = 1.0
    bones[64:, 64:] = 1.0
    p = np.arange(128)
    partner = np.where((p % 64) < 32, p + 32, p - 32)
    rperm = np.zeros((128, 128), np.float32)
    rperm[partner, p] = 1.0
    kk = np.arange(128)[:, None]
    qq = np.arange(128)[None, :]
    tri = (qq >= kk).astype(np.float32)
    inv = (1.0 / (np.float32(10000.0) ** (np.arange(0, 64, 2, dtype=np.float32) / np.float32(64))))
    inv = inv.astype(np.float32)
    pos = np.arange(T, dtype=np.float32)
    ang = (pos[:, None] * inv[None, :]).astype(np.float32)
    cosv = np.cos(ang).astype(np.float32)
    sinv = np.sin(ang).astype(np.float32)
    f = p % 32
    cos_t = np.ascontiguousarray(cosv[:, f].T)
    sgn = np.where((p % 64) < 32, -1.0, 1.0).astype(np.float32)
    sin_t = np.ascontiguousarray((sinv[:, f] * sgn[None, :]).T)
    return dict(c_ident=ident, c_ones=ones, c_blockones=bones, c_rperm=rperm, c_tri=tri,
                c_cos=cos_t.astype(np.float32), c_sin=sin_t.astype(np.float32))


_CACHE = {}


def run(inputs, NSEQ, NT, n_cores=N_CORES, **flags):
    key = (NSEQ, NT, tuple(sorted(flags.items())))
    if key not in _CACHE:
        _CACHE[key] = build_program(NSEQ, NT, **flags)
    nc = _CACHE[key][0]
    T = NMETA + 512 * NT
    f32 = lambda a: np.ascontiguousarray(np.asarray(a, dtype=np.float32))
    shared = dict(
        meta_tokens=f32(inputs["meta_tokens"]),
        mixer_norm_g=f32(inputs["mixer_norm_g"]),
        ffn_norm_g=f32(inputs["ffn_norm_g"]),
        attn_w_qkv=f32(inputs["attn_w_qkv"][0]),
        attn_q_gain=f32(inputs["attn_q_gain"]),
        attn_k_gain=f32(inputs["attn_k_gain"]),
        attn_lambda_q1=f32(inputs["attn_lambda_q1"]),
        attn_lambda_k1=f32(inputs["attn_lambda_k1"]),
        attn_lambda_q2=f32(inputs["attn_lambda_q2"]),
        attn_lambda_k2=f32(inputs["attn_lambda_k2"]),
        attn_sub_gain=f32(inputs["attn_sub_gain"]),
        attn_w_o=f32(inputs["attn_w_o"][0]),
        conv_w_in=f32(inputs["conv_w_in"][0]),
        conv_w=f32(np.asarray(inputs["conv_w"]).reshape(3, D)),
        conv_w_out=f32(inputs["conv_w_out"][0]),
        ffn_w_gate_up=f32(inputs["ffn_w_gate_up"]),
        ffn_w_down=f32(inputs["ffn_w_down"]),
    )
    shared.update(make_consts(T))
    x = f32(inputs["x"])
    in_maps = []
    for c in range(n_cores):
        m = dict(shared)
        m["x"] = np.ascontiguousarray(x[c * NSEQ:(c + 1) * NSEQ])
        in_maps.append(m)
    res = run_bass_kernel_spmd(nc, in_maps, core_ids=list(range(n_cores)))
    return np.concatenate([np.asarray(r["out"]) for r in res.results], axis=0)


def kernel(**inputs):
    out = run(inputs, NSEQ=4, NT=4)
    return out.astype(np.float32)
```
